# Optimizing a Trainium2 kernel written in Bass

```python
import math
import jax, jax.numpy as jnp
from jax import lax
import numpy as np

D_MODEL = 1024
BATCH = 4
SEQ = 8192
DEPTH = 2

M_HEADS = 4
M_HEAD_DIM = 128
M_WIDTH = M_HEADS * M_HEAD_DIM
H_HEADS = 4
H_KEY_DIM = 128
H_VAL_DIM = 128
H_KWIDTH = H_HEADS * H_KEY_DIM
H_VWIDTH = H_HEADS * H_VAL_DIM
R_WIDTH = 512
R_BLOCKS = 4
R_BLOCK_DIM = R_WIDTH // R_BLOCKS
R_GATE_C = 8.0
CONV_WIDTH = 4
CHUNK = 64
N_BRANCHES = 3
D_FF = -(-8 * D_MODEL // (3 * 256)) * 256
DN_ALPHA = (2 * DEPTH) ** 0.25
DN_BETA = (8 * DEPTH) ** -0.25
LN_EPS = 1e-5
NORM_EPS = 1e-6
IN_SPLITS = (M_WIDTH, M_WIDTH, M_WIDTH, M_WIDTH, M_HEADS, M_HEADS,
             H_KWIDTH, H_KWIDTH, H_VWIDTH, H_VWIDTH, R_WIDTH, R_WIDTH,
             N_BRANCHES * D_MODEL)
D_IN = sum(IN_SPLITS)

kernel_name = "hybrid_mlstm_hgrn2_rglru_deepnorm"


def layer_norm(x, g, b):
    xf = x.astype(jnp.float32)
    mu = jnp.mean(xf, -1, keepdims=True)
    var = jnp.mean(jnp.square(xf - mu), -1, keepdims=True)
    return ((xf - mu) * lax.rsqrt(var + LN_EPS) * g + b).astype(x.dtype)


def head_rms_norm(h, g):
    h = h * lax.rsqrt(jnp.mean(jnp.square(h), -1, keepdims=True) + NORM_EPS)
    return h.reshape(h.shape[0], h.shape[1], -1) * g.astype(jnp.float32)


def causal_depthwise_conv(x, w, b):
    y = lax.conv_general_dilated(
        x, w[:, None, :].astype(x.dtype), window_strides=(1,),
        padding=((CONV_WIDTH - 1, 0),), dimension_numbers=('NWC', 'WIO', 'NWC'),
        feature_group_count=x.shape[-1])
    return y + b


def split_heads(t, d):
    b, s = t.shape[:2]
    return t.reshape(b, s, -1, d).transpose(0, 2, 1, 3).astype(jnp.float32)


def to_chunks(t):
    b, h, s = t.shape[:3]
    return jnp.moveaxis(t.reshape(b, h, s // CHUNK, CHUNK, *t.shape[3:]), 2, 0)


def from_chunks(t):
    n, b, h, c = t.shape[:4]
    return jnp.moveaxis(t, 0, 2).reshape(b, h, n * c, *t.shape[4:])


def mlstm_chunkwise(q, k, v, log_i, log_f):
    bsz, nh, _, d = q.shape
    causal = jnp.tril(jnp.ones((CHUNK, CHUNK), bool))

    def step(carry, inp):
        c_st, n_st, m_st = carry
        q_, k_, v_, li, lf = inp
        b = jnp.cumsum(lf, axis=-1)
        log_d = jnp.where(causal, b[..., :, None] - b[..., None, :] + li[..., None, :], -jnp.inf)
        log_inter = b + m_st[..., None]
        m_t = jnp.maximum(jnp.max(log_d, -1), log_inter)
        w_intra = jnp.exp(log_d - m_t[..., None])
        w_inter = jnp.exp(log_inter - m_t)
        scores = jnp.einsum('bhtd,bhsd->bhts', q_, k_) * w_intra
        num = (jnp.einsum('bhts,bhse->bhte', scores, v_)
               + w_inter[..., None] * jnp.einsum('bhtd,bhde->bhte', q_, c_st))
        den = jnp.sum(scores, -1) + w_inter * jnp.einsum('bhtd,bhd->bht', q_, n_st)
        h = num / jnp.maximum(jnp.abs(den), jnp.exp(-m_t))[..., None]
        log_s = b[..., -1:] - b + li
        log_c = b[..., -1] + m_st
        m_new = jnp.maximum(jnp.max(log_s, -1), log_c)
        w_s = jnp.exp(log_s - m_new[..., None])
        w_c = jnp.exp(log_c - m_new)
        c_new = w_c[..., None, None] * c_st + jnp.einsum('bhs,bhsd,bhse->bhde', w_s, k_, v_)
        n_new = w_c[..., None] * n_st + jnp.einsum('bhs,bhsd->bhd', w_s, k_)
        return (c_new, n_new, m_new), h

    init = (jnp.zeros((bsz, nh, d, d), jnp.float32), jnp.zeros((bsz, nh, d), jnp.float32),
            jnp.zeros((bsz, nh), jnp.float32))
    _, hs = lax.scan(step, init, tuple(map(to_chunks, (q, k, v, log_i, log_f))))
    return from_chunks(hs)


def hgrn2_chunkwise(q, k, v, log_f):
    bsz, nh, _, dk = q.shape
    dv = v.shape[-1]
    causal = jnp.tril(jnp.ones((CHUNK, CHUNK), bool))[..., None]

    def step(s_st, inp):
        q_, k_, v_, lf = inp
        b = jnp.cumsum(lf, axis=-2)
        decay = jnp.exp(jnp.where(causal, b[..., :, None, :] - b[..., None, :, :], -jnp.inf))
        a = jnp.einsum('bhtd,bhsd,bhtsd->bhts', q_, k_, decay)
        o = (jnp.einsum('bhts,bhse->bhte', a, v_)
             + jnp.einsum('bhtd,bhde->bhte', q_ * jnp.exp(b), s_st))
        b_last = b[..., -1:, :]
        s_new = (jnp.exp(b_last[..., 0, :])[..., None] * s_st
                 + jnp.einsum('bhsd,bhse->bhde', k_ * jnp.exp(b_last - b), v_))
        return s_new, o

    init = jnp.zeros((bsz, nh, dk, dv), jnp.float32)
    _, os_ = lax.scan(step, init, tuple(map(to_chunks, (q, k, v, log_f))))
    return from_chunks(os_)


def linear_combine(left, right):
    a_l, u_l = left
    a_r, u_r = right
    return a_l * a_r, a_r * u_l + u_r


def rg_lru(u, w_rec, b_rec, w_inp, b_inp, lam):
    bsz, seq, width = u.shape
    uf = u.astype(jnp.float32)
    ub = uf.reshape(bsz, seq, R_BLOCKS, R_BLOCK_DIM)
    r = jax.nn.sigmoid(jnp.einsum('bsni,nij->bsnj', ub, w_rec).reshape(bsz, seq, width) + b_rec)
    i = jax.nn.sigmoid(jnp.einsum('bsni,nij->bsnj', ub, w_inp).reshape(bsz, seq, width) + b_inp)
    log_a = -R_GATE_C * r * jax.nn.softplus(-lam.astype(jnp.float32))
    a = jnp.exp(log_a)
    gated = jnp.sqrt(-jnp.expm1(2.0 * log_a)) * (i * uf)
    _, h = lax.associative_scan(linear_combine, (a, gated), axis=1)
    return h.astype(u.dtype)


def token_mixer(x, w_in, m_conv_w, m_conv_b, m_bias_i, m_bias_f, m_norm_g, lower_bound,
                h_norm_g, r_conv_w, r_conv_b, r_w_rec, r_b_rec, r_w_in, r_b_in, r_lambda,
                w_branch_m, w_branch_h, w_branch_r, w_out):
    f32 = jnp.float32
    bsz, seq, _ = x.shape
    proj = x @ w_in
    idx = np.cumsum(IN_SPLITS)[:-1].tolist()
    mq, mk, mv, mo, mi, mf, hq, hf, hi, hg, rx, rg, gates = jnp.split(proj, idx, axis=-1)

    qk = jax.nn.silu(causal_depthwise_conv(jnp.concatenate([mq, mk], -1), m_conv_w, m_conv_b))
    mq, mk = jnp.split(qk, 2, axis=-1)
    log_i = (mi + m_bias_i).astype(f32).transpose(0, 2, 1)
    log_f = jax.nn.log_sigmoid((mf + m_bias_f).astype(f32)).transpose(0, 2, 1)
    h_m = mlstm_chunkwise(split_heads(mq, M_HEAD_DIM), split_heads(mk, M_HEAD_DIM) * M_HEAD_DIM ** -0.5,
                          split_heads(mv, M_HEAD_DIM), log_i, log_f)
    h_m = h_m.transpose(0, 2, 1, 3) * jax.nn.sigmoid(mo.astype(f32)).reshape(bsz, seq, M_HEADS, M_HEAD_DIM)
    y_m = head_rms_norm(h_m, m_norm_g).astype(x.dtype)

    z = hf.astype(f32)
    lb = lower_bound.astype(f32)
    log_f_h = jnp.logaddexp(jnp.log(lb), jnp.log1p(-lb) + jax.nn.log_sigmoid(z))
    k_h = (1.0 - lb) * jax.nn.sigmoid(-z)
    o_h = hgrn2_chunkwise(split_heads(jax.nn.silu(hq), H_KEY_DIM), split_heads(k_h, H_KEY_DIM),
                          split_heads(hi, H_VAL_DIM), split_heads(log_f_h, H_KEY_DIM))
    y_h = (head_rms_norm(o_h.transpose(0, 2, 1, 3), h_norm_g)
           * jax.nn.sigmoid(hg.astype(f32))).astype(x.dtype)

    u = causal_depthwise_conv(rx, r_conv_w, r_conv_b)
    y_r = rg_lru(u, r_w_rec, r_b_rec, r_w_in, r_b_in, r_lambda) * jax.nn.gelu(rg)

    g_m, g_h, g_r = jnp.split(jax.nn.sigmoid(gates), N_BRANCHES, axis=-1)
    mixed = g_m * (y_m @ w_branch_m) + g_h * (y_h @ w_branch_h) + g_r * (y_r @ w_branch_r)
    return mixed @ w_out


def swiglu(x, w_gate, w_up, w_down):
    return (jax.nn.silu(x @ w_gate) * (x @ w_up)) @ w_down


def setup_inputs(seed: int = 0) -> dict:
    key = jax.random.key(seed)
    ks = jax.random.split(key, 32)
    L = DEPTH

    def nrm(k, shape, scale):
        return jax.random.normal(k, shape, jnp.float32) * scale

    u = jax.random.uniform(ks[14], (L, R_WIDTH), jnp.float32, 0.9, 0.999)
    a0 = u ** (1.0 / R_GATE_C)
    return {
        "x": nrm(ks[0], (BATCH, SEQ, D_MODEL), 1.0),
        "w_in": nrm(ks[1], (L, D_MODEL, D_IN), D_MODEL ** -0.5),
        "m_conv_w": nrm(ks[2], (L, CONV_WIDTH, 2 * M_WIDTH), CONV_WIDTH ** -0.5),
        "m_conv_b": nrm(ks[3], (L, 2 * M_WIDTH), 0.01),
        "m_bias_i": nrm(ks[4], (L, M_HEADS), 0.1),
        "m_bias_f": jnp.linspace(3.0, 6.0, M_HEADS)[None, :] + nrm(ks[5], (L, M_HEADS), 0.1),
        "m_norm_g": 1.0 + nrm(ks[6], (L, M_WIDTH), 0.02),
        "h_lower_bounds": nrm(ks[7], (L, H_KWIDTH), 0.1),
        "h_norm_g": 1.0 + nrm(ks[8], (L, H_VWIDTH), 0.02),
        "r_conv_w": nrm(ks[9], (L, CONV_WIDTH, R_WIDTH), CONV_WIDTH ** -0.5),
        "r_conv_b": nrm(ks[10], (L, R_WIDTH), 0.01),
        "r_w_rec": nrm(ks[11], (L, R_BLOCKS, R_BLOCK_DIM, R_BLOCK_DIM), R_BLOCK_DIM ** -0.5),
        "r_b_rec": nrm(ks[12], (L, R_WIDTH), 0.01),
        "r_w_in": nrm(ks[13], (L, R_BLOCKS, R_BLOCK_DIM, R_BLOCK_DIM), R_BLOCK_DIM ** -0.5),
        "r_b_in": nrm(ks[15], (L, R_WIDTH), 0.01),
        "r_lambda": jnp.log(a0) - jnp.log1p(-a0),
        "w_branch_m": nrm(ks[16], (L, M_WIDTH, D_MODEL), DN_BETA * M_WIDTH ** -0.5),
        "w_branch_h": nrm(ks[17], (L, H_VWIDTH, D_MODEL), DN_BETA * H_VWIDTH ** -0.5),
        "w_branch_r": nrm(ks[18], (L, R_WIDTH, D_MODEL), DN_BETA * R_WIDTH ** -0.5),
        "w_out": nrm(ks[19], (L, D_MODEL, D_MODEL), DN_BETA * D_MODEL ** -0.5),
        "ln1_g": 1.0 + nrm(ks[20], (L, D_MODEL), 0.02),
        "ln1_b": nrm(ks[21], (L, D_MODEL), 0.01),
        "w_ff_gate": nrm(ks[22], (L, D_MODEL, D_FF), DN_BETA * D_MODEL ** -0.5),
        "w_ff_up": nrm(ks[23], (L, D_MODEL, D_FF), DN_BETA * D_MODEL ** -0.5),
        "w_ff_down": nrm(ks[24], (L, D_FF, D_MODEL), DN_BETA * D_FF ** -0.5),
        "ln2_g": 1.0 + nrm(ks[25], (L, D_MODEL), 0.02),
        "ln2_b": nrm(ks[26], (L, D_MODEL), 0.01),
    }


def reference(x, w_in, m_conv_w, m_conv_b, m_bias_i, m_bias_f, m_norm_g, h_lower_bounds,
              h_norm_g, r_conv_w, r_conv_b, r_w_rec, r_b_rec, r_w_in, r_b_in, r_lambda,
              w_branch_m, w_branch_h, w_branch_r, w_out, ln1_g, ln1_b, w_ff_gate, w_ff_up,
              w_ff_down, ln2_g, ln2_b):
    lb_all = jnp.cumsum(jax.nn.softmax(h_lower_bounds.astype(jnp.float32), axis=0), axis=0)
    lb_all = lb_all - lb_all[0]
    for l in range(DEPTH):
        mix = token_mixer(x, w_in[l], m_conv_w[l], m_conv_b[l], m_bias_i[l], m_bias_f[l], m_norm_g[l],
                          lb_all[l], h_norm_g[l], r_conv_w[l], r_conv_b[l], r_w_rec[l], r_b_rec[l],
                          r_w_in[l], r_b_in[l], r_lambda[l], w_branch_m[l], w_branch_h[l],
                          w_branch_r[l], w_out[l])
        x = layer_norm(DN_ALPHA * x + mix, ln1_g[l], ln1_b[l])
        x = layer_norm(DN_ALPHA * x + swiglu(x, w_ff_gate[l], w_ff_up[l], w_ff_down[l]), ln2_g[l], ln2_b[l])
    return x
```

```python
import math
from contextlib import ExitStack
import numpy as np
import concourse.bass as bass
import concourse.mybir as mybir
from concourse.bass_utils import run_bass_kernel_spmd

F32, BF16 = mybir.dt.float32, mybir.dt.bfloat16
AF = mybir.ActivationFunctionType
ALU = mybir.AluOpType
AX = mybir.AxisListType

D = 1024
DIN = 8200
DFF = 2816
T = 512
ALPHA = 4.0 ** 0.25
LN_EPS = 1e-5
NORM_EPS = 1e-6
NW = 4
WSLOT = 4160
NCV = 112
SAME_ENG_WIN = 8


class V:
    __slots__ = ("t", "name", "lo", "hi", "p0", "p1", "fn")

    def __init__(s, t, name, lo, hi, p0=0, p1=128, fn=None):
        s.t, s.name, s.lo, s.hi, s.p0, s.p1, s.fn = t, name, lo, hi, p0, p1, fn

    @property
    def ap(s):
        a = s.t[s.p0:s.p1, s.lo:s.hi]
        if s.fn is not None:
            a = s.fn(a)
        return a


class Buf:
    def __init__(s, t, name):
        s.t, s.name = t, name

    def v(s, lo, n, p0=0, p1=128, fn=None):
        return V(s.t, s.name, lo, lo + n, p0, p1, fn)


class Sched:
    ENG = ("pe", "act", "dve", "pool", "sp")

    def __init__(s):
        s.ops = {e: [] for e in s.ENG}
        s.cnt = {e: 0 for e in s.ENG}
        s.waited = {e: {} for e in s.ENG}
        s.reg = {}
        s.dmacnt = {}
        s.semkeys = set(s.ENG)

    def _collect(s, need, eng, tok):
        if tok is None:
            return
        k, v = tok
        if k == eng:
            if eng == "pe" or v > s.cnt[eng] or v < s.cnt[eng] - SAME_ENG_WIN:
                return
        if s.waited[eng].get(k, 0) >= v:
            return
        if need.get(k, 0) < v:
            need[k] = v

    def _deps(s, eng, reads, writes, tok):
        need = {}
        for r in reads:
            for ent in s.reg.get(r.name, ()):
                if ent[0] < r.hi and r.lo < ent[1]:
                    s._collect(need, eng, ent[2])
        for w in writes:
            for ent in s.reg.get(w.name, ()):
                if ent[0] < w.hi and w.lo < ent[1]:
                    s._collect(need, eng, ent[2])
                    for k, v in ent[3].items():
                        s._collect(need, eng, (k, v))
        for k, v in need.items():
            s.waited[eng][k] = v
        return sorted(need.items())

    def _record(s, reads, writes, tok):
        for r in reads:
            lst = s.reg.setdefault(r.name, [])
            hit = False
            for ent in lst:
                if ent[0] < r.hi and r.lo < ent[1]:
                    if ent[3].get(tok[0], 0) < tok[1]:
                        ent[3][tok[0]] = tok[1]
                    if ent[0] <= r.lo and r.hi <= ent[1]:
                        hit = True
            if not hit:
                lst.append([r.lo, r.hi, None, {tok[0]: tok[1]}])
        for w in writes:
            lst = s.reg.setdefault(w.name, [])
            lst[:] = [ent for ent in lst if not (w.lo <= ent[0] and ent[1] <= w.hi)]
            lst.append([w.lo, w.hi, tok, {}])

    @staticmethod
    def _bank(reads, writes):
        r2, w2 = [], []
        for r in reads:
            if r.name[0] == "P":
                w2.append(V(None, r.name, 0, 1 << 30))
            else:
                r2.append(r)
        for w in writes:
            w2.append(V(None, w.name, 0, 1 << 30) if w.name[0] == "P" else w)
        return r2, w2

    def op(s, eng, fn, reads=(), writes=(), sig=True):
        reads, writes = s._bank(reads, writes)
        waits = s._deps(eng, reads, writes, None)
        if sig:
            s.cnt[eng] += 1
            tok = (eng, s.cnt[eng])
            inc = (eng, s.cnt[eng])
        else:
            tok = (eng, s.cnt[eng] + 1)
            inc = None
        s._record(reads, writes, tok)
        s.ops[eng].append((waits, fn, inc))

    def dma(s, eng, fn, reads, writes, semkey):
        waits = s._deps(eng, reads, writes, None)
        s.semkeys.add(semkey)
        s.dmacnt[semkey] = s.dmacnt.get(semkey, 0) + 16
        tok = (semkey, s.dmacnt[semkey])
        s._record(reads, writes, tok)
        s.ops[eng].append((waits, fn, (semkey, 16)))

    def wait_all(s, eng, toks):
        s.ops[eng].append((sorted(toks), None, None))


class _Stop(Exception):
    pass


def build_program(S_len, n_layers=2, stage=99):
    assert S_len % T == 0
    ntiles = S_len // T
    nc = bass.Bass("TRN2", target_bir_lowering=False)
    es = ExitStack()
    S = Sched()

    def dram(name, shape, dt, kind):
        return nc.dram_tensor(name, list(shape), dt, kind=kind).ap()

    xT_d = dram("xT", [D, S_len], F32, "ExternalInput")
    outT_d = dram("outT", [D, S_len], F32, "ExternalOutput")
    w_in_d = dram("w_in", [n_layers, D, DIN], F32, "ExternalInput")
    wbm_d = dram("w_branch_m", [n_layers, 512, D], F32, "ExternalInput")
    wbh_d = dram("w_branch_h", [n_layers, 512, D], F32, "ExternalInput")
    wbr_d = dram("w_branch_r", [n_layers, 512, D], F32, "ExternalInput")
    wout_d = dram("w_out", [n_layers, D, D], F32, "ExternalInput")
    wg_d = dram("w_ff_gate", [n_layers, D, DFF], F32, "ExternalInput")
    wu_d = dram("w_ff_up", [n_layers, D, DFF], F32, "ExternalInput")
    wd_d = dram("w_ff_down", [n_layers, DFF, D], F32, "ExternalInput")
    cvec_d = dram("cvec", [n_layers, 128, NCV], F32, "ExternalInput")
    brow_d = dram("brow", [n_layers, 128, 1056], F32, "ExternalInput")
    rw_d = dram("rw", [n_layers, 128, 1024], F32, "ExternalInput")
    cmat_d = dram("cmat", [128, 512], F32, "ExternalInput")

    wspec = [("win", w_in_d, 8, DIN), ("wbm", wbm_d, 4, D), ("wbh", wbh_d, 4, D), ("wbr", wbr_d, 4, D),
             ("wout", wout_d, 8, D), ("wg", wg_d, 8, DFF), ("wu", wu_d, 8, DFF), ("wd", wd_d, 22, D)]
    scr = {}
    for l in range(n_layers):
        for (nm, src, k, n) in wspec:
            t = nc.dram_tensor(f"s_{nm}{l}", [128, k * n], BF16, kind="Internal").ap()
            scr[(nm, l)] = (t, f"s_{nm}{l}", k, n, src)

    def sb(name, n, dt=F32):
        t = es.enter_context(nc.sbuf_tensor(name, [128, n], dt))
        return Buf(t, name)

    XT = sb("XT", 8 * T)
    XB = sb("XB", 8 * T, BF16)
    WB = [sb(f"WB{i}", WSLOT, BF16) for i in range(NW)]
    NTM = 12
    TM = [sb(f"TM{i}", 516) for i in range(NTM)]
    MIX = sb("MIX", 8 * T)
    MB = sb("MBF", 22 * T, BF16)
    YM = sb("YMT", 4 * T, BF16)
    YH = sb("YHT", 4 * T, BF16)
    YR = sb("YRT", 4 * T, BF16)
    MIXB = sb("MIXB", 8 * T, BF16)
    CST = sb("CST", 512)
    IDB = sb("IDB", 128, BF16)
    ONES = sb("ONES", 512)
    CV = [sb(f"CV{l}", NCV) for l in range(n_layers)]
    BR = [sb(f"BR{l}", 1056) for l in range(n_layers)]
    RW = [sb(f"RW{l}", 1024, BF16) for l in range(n_layers)]
    LC = [sb(f"LC{l}", 32) for l in range(n_layers)]
    CS = [sb(f"CS{l}", 4 * 129) for l in range(n_layers)]
    CSB = [sb(f"CSB{l}", 4 * 129, BF16) for l in range(n_layers)]
    HS = [sb(f"HS{l}", 4 * 128) for l in range(n_layers)]
    HALM = [sb(f"HALM{l}", 8 * 3) for l in range(n_layers)]
    HALR = [sb(f"HALR{l}", 4 * 3) for l in range(n_layers)]
    RST = [sb(f"RST{l}", 4) for l in range(n_layers)]
    SM = sb("SM", 512)
    SMB = sb("SMB", 1024, BF16)

    PS = [Buf(es.enter_context(nc.psum_tensor(f"PS{i}", [128, 512], F32)), f"PS{i}") for i in range(6)]
    PT = [Buf(es.enter_context(nc.psum_tensor(f"PT{i}", [128, 1024], BF16)), f"PT{i}") for i in range(2)]
    rr = {"ps": 0, "pt": 0, "w": 0, "tm": 0}

    def psum():
        rr["ps"] = (rr["ps"] + 1) % 6
        return PS[rr["ps"]]

    def psumT():
        rr["pt"] = (rr["pt"] + 1) % 2
        return PT[rr["pt"]]

    def mb(lo, n, **kw):
        return MB.v(lo, n, **kw)
    QT_O, KT_O, VA_O, OG_O, YMK_O = 0, 2048, 4096, 4096 + 2064, 4096 + 2064 + 2048
    QH_O, KH_O, VH_O, SGH_O, YHK_O = 0, 2048, 4096, 8192, 10240
    YHK = sb("YHK", 8 * T, BF16)

    maskU = CST.v(0, 128)
    identF = CST.v(128, 128)
    onesM = CST.v(256, 128)
    ones1 = CST.v(384, 128)

    def mm(out, lhsT, rhs, start=True, stop=True):
        S.op("pe", lambda e: e.matmul(out.ap, lhsT=lhsT.ap, rhs=rhs.ap, start=start, stop=stop),
             reads=[lhsT, rhs], writes=[out], sig=stop)

    def tr(out, in_, ident):
        S.op("pe", lambda e: e.transpose(out.ap, in_.ap, ident.ap), reads=[in_, ident], writes=[out])

    def act(out, in_, func, bias=None, scale=None):
        rd = [in_]
        kw = {}
        if bias is not None:
            if isinstance(bias, V):
                rd.append(bias)
            kw["bias"] = bias
        if scale is not None:
            if isinstance(scale, V):
                rd.append(scale)
            kw["scale"] = scale

        def f(e):
            k2 = {k: (v.ap if isinstance(v, V) else v) for k, v in kw.items()}
            return e.activation(out=out.ap, in_=in_.ap, func=func, **k2)
        S.op("act", f, reads=rd, writes=[out])

    def ts(out, in0, s1, s2, op0, op1=None, eng="dve"):
        rd = [in0] + [x for x in (s1, s2) if isinstance(x, V)]

        def f(e):
            a1 = s1.ap if isinstance(s1, V) else s1
            a2 = s2.ap if isinstance(s2, V) else s2
            if op1 is None:
                return e.tensor_scalar(out=out.ap, in0=in0.ap, scalar1=a1, scalar2=None, op0=op0)
            return e.tensor_scalar(out=out.ap, in0=in0.ap, scalar1=a1, scalar2=a2, op0=op0, op1=op1)
        S.op(eng, f, reads=rd, writes=[out])

    def stt(out, in0, scalar, in1, op0, op1):
        rd = [in0, in1] + ([scalar] if isinstance(scalar, V) else [])

        def f(e):
            sc = scalar.ap if isinstance(scalar, V) else scalar
            return e.scalar_tensor_tensor(out=out.ap, in0=in0.ap, scalar=sc, in1=in1.ap, op0=op0, op1=op1)
        S.op("dve", f, reads=rd, writes=[out])

    def tt(out, in0, in1, op, eng="dve"):
        S.op(eng, lambda e: e.tensor_tensor(out=out.ap, in0=in0.ap, in1=in1.ap, op=op), reads=[in0, in1], writes=[out])

    def scan(out, d0, d1, init, op0, op1):
        rd = [d0, d1] + ([init] if isinstance(init, V) else [])

        def f(e):
            i = init.ap if isinstance(init, V) else init
            return e.tensor_tensor_scan(out=out.ap, data0=d0.ap, data1=d1.ap, initial=i, op0=op0, op1=op1)
        S.op("dve", f, reads=rd, writes=[out])

    def recip(out, in_):
        S.op("dve", lambda e: e.reciprocal(out=out.ap, in_=in_.ap), reads=[in_], writes=[out])

    def copy(out, in_, eng="dve"):
        S.op(eng, lambda e: e.tensor_copy(out=out.ap, in_=in_.ap), reads=[in_], writes=[out])

    def rsum(out, in_):
        S.op("dve", lambda e: e.reduce_sum(out=out.ap, in_=in_.ap, axis=AX.X), reads=[in_], writes=[out])

    def memset(buf_v, val, eng="dve"):
        S.op(eng, lambda e: e.memset(buf_v.ap, val), reads=[], writes=[buf_v])

    def dma(eng, out, in_, semkey):
        S.dma(eng, lambda e: e.dma_start(out=out.ap, in_=in_.ap), reads=[in_], writes=[out], semkey=semkey)

    def dview(ap, name, n, fn=None):
        return V(ap, name, 0, n, fn=fn)

    for l in range(n_layers):
        if stage < 0:
            break
        for (nm, src, k, n) in wspec:
            t, tname, k, n, src = scr[(nm, l)]
            semkey = f"cast_{nm}{l}"
            for kc in range(k):
                for c0 in range(0, n, 2048):
                    c1 = min(n, c0 + 2048)
                    sap = src[l][kc * 128:(kc + 1) * 128, c0:c1]
                    dst = V(t, tname, kc * n + c0, kc * n + c1)
                    S.dma("pool", (lambda e, dst=dst, sap=sap: e.dma_start(out=dst.ap, in_=sap)), reads=[], writes=[dst],
                          semkey=semkey)
            S.reg[tname] = [[0, k * n, (semkey, S.dmacnt[semkey]), {}]]
    S.dma("sp", lambda e: e.dma_start(out=CST.t[:, :], in_=cmat_d), reads=[], writes=[CST.v(0, 512)], semkey="c0")
    for l in range(n_layers):
        S.dma("sp", (lambda e, l=l: e.dma_start(out=CV[l].t[:, :], in_=cvec_d[l])), reads=[], writes=[CV[l].v(0, NCV)], semkey=f"c1_{l}")
        S.dma("sp", (lambda e, l=l: e.dma_start(out=BR[l].t[:, :], in_=brow_d[l])), reads=[], writes=[BR[l].v(0, 1056)], semkey=f"c2_{l}")
        S.dma("pool", (lambda e, l=l: e.dma_start(out=RW[l].t[:, :], in_=rw_d[l])), reads=[], writes=[RW[l].v(0, 1024)], semkey=f"c3_{l}")
    S.dma("pool", lambda e: e.dma_start(out=IDB.t[:, :], in_=cmat_d[:, 128:256]), reads=[], writes=[IDB.v(0, 128)], semkey="c4")
    identB = IDB.v(0, 128)
    CB = sb("CB", 8)
    memset(CB.v(0, 1), 1.0)
    memset(CB.v(1, 1), math.log(128.0 ** -0.5))
    memset(CB.v(2, 1), LN_EPS)
    memset(CB.v(3, 1), NORM_EPS)
    B_ONE, B_LNS, B_LNEPS, B_NEPS = CB.v(0, 1), CB.v(1, 1), CB.v(2, 1), CB.v(3, 1)
    memset(ONES.v(0, 512), 1.0)
    for l in range(n_layers):
        memset(CS[l].v(0, 516), 0.0)
        memset(CSB[l].v(0, 516), 0.0)
        memset(HS[l].v(0, 512), 0.0)
        memset(HALM[l].v(0, 24), 0.0)
        memset(HALR[l].v(0, 12), 0.0)
        memset(RST[l].v(0, 4), 0.0)
    e0 = SM.v(0, 4)
    e1 = SM.v(4, 4)
    esum = SM.v(8, 4)
    act(e0, CV[0].v(104, 4), AF.Exp)
    act(e1, CV[0].v(108, 4), AF.Exp)
    tt(esum, e0, e1, ALU.add)
    recip(esum, esum)
    memset(LC[0].v(0, 4), 0.0)
    if n_layers > 1:
        tt(LC[1].v(0, 4), e1, esum, ALU.mult)
    for l in range(n_layers):
        ts(LC[l].v(4, 4), LC[l].v(0, 4), -1.0, 1.0, ALU.mult, ALU.add)
        act(SM.v(12, 4), CV[l].v(68, 4), AF.Exp, scale=-1.0)
        act(SM.v(16, 4), SM.v(12, 4), AF.Ln, bias=B_ONE)
        ts(LC[l].v(8, 4), SM.v(16, 4), -8.0, None, ALU.mult)
        ts(LC[l].v(12, 4), SM.v(16, 4), 8.0, None, ALU.mult)

    def wload(nm, l, c0, n):
        t, tname, k, ntot, _ = scr[(nm, l)]
        rr["w"] = (rr["w"] + 1) % NW
        slot = WB[rr["w"]]
        src = V(t, tname, 0, k * ntot, fn=lambda a: a.rearrange("p (k n) -> p k n", k=k)[:, :, c0:c0 + n])
        dst = V(slot.t, slot.name, 0, k * n, fn=lambda a: a.rearrange("p (k n) -> p k n", k=k))
        dma("sp", dst, src, f"w{rr['w']}")
        return slot

    def tm():
        rr["tm"] = (rr["tm"] + 1) % NTM
        return TM[rr["tm"]]

    LNS = math.log(128.0 ** -0.5)

    def chk(k):
        if stage < k:
            raise _Stop()

    def proj_fm(slot, n, j, nk, rhs_of_k, evac):
        ps = psum()
        for kc in range(nk):
            mm(ps.v(0, T), slot.v(kc * n + j * 128, 128), rhs_of_k(kc), start=(kc == 0), stop=(kc == nk - 1))
        evac(ps.v(0, T))

    xbk = lambda kc: XB.v(kc * T, T)

    def layer_norm(l, goff, boff):
        s1 = psum()
        s2 = psum()
        for j in range(8):
            zq = tm()
            act(zq.v(0, T), XT.v(j * T, T), AF.Square)
            mm(s1.v(0, T), onesM, XT.v(j * T, T), start=(j == 0), stop=(j == 7))
            mm(s2.v(0, T), onesM, zq.v(0, T), start=(j == 0), stop=(j == 7))
        mean = tm().v(0, T)
        var = tm().v(0, T)
        rstd = tm().v(0, T)
        nmr = tm().v(0, T)
        act(mean, s1.v(0, T), AF.Copy)
        tt(var, mean, mean, ALU.mult)
        tt(var, s2.v(0, T), var, ALU.subtract)
        act(var, var, AF.Sqrt, bias=B_LNEPS)
        recip(rstd, var)
        stt(nmr, mean, -1.0, rstd, ALU.mult, ALU.mult)
        for j in range(8):
            t1 = tm().v(0, T)
            g = CV[l].v(goff + j, 1)
            b = CV[l].v(boff + j, 1)
            stt(t1, XT.v(j * T, T), g, rstd, ALU.mult, ALU.mult)
            stt(t1, nmr, g, t1, ALU.mult, ALU.add)
            ts(XT.v(j * T, T), t1, b, None, ALU.add)
            act(XB.v(j * T, T), t1, AF.Identity, bias=b)

    for ti in range(ntiles):
        t0 = ti * T
        xsrc = V(xT_d, "xT_d", 0, 1, fn=lambda a, t0=t0: a.rearrange("(k p) s -> p k s", p=128)[:, :, t0:t0 + T])
        xsrc.fn = xsrc.fn
        xdst = V(XT.t, XT.name, 0, 8 * T, fn=lambda a: a.rearrange("p (k n) -> p k n", k=8))
        S.dma("pool", (lambda e, xdst=xdst, t0=t0: e.dma_start(
            out=xdst.ap, in_=xT_d.rearrange("(k p) s -> p k s", p=128)[:, :, t0:t0 + T])),
            reads=[], writes=[xdst], semkey=f"x{ti % 2}")
        for j in range(8):
            act(XB.v(j * T, T), XT.v(j * T, T), AF.Copy)

        for l in range(n_layers):
          try:
            cv = CV[l]
            chk(1)
            w = wload("win", l, 1536, 520)
            OG = lambda tb, h: mb(OG_O + tb * 512 + h * 128, 128)
            for tb in range(4):
                ps = psum()
                for kc in range(8):
                    mm(ps.v(0, 512), XB.v(kc * T + tb * 128, 128), w.v(kc * 520, 512), start=(kc == 0), stop=(kc == 7))
                act(mb(OG_O + tb * 512, 512), ps.v(0, 512), AF.Sigmoid)
            psg = psum()
            for tb in range(4):
                for kc in range(8):
                    mm(psg.v(tb * 8, 8), XB.v(kc * T + tb * 128, 128), w.v(kc * 520 + 512, 8), start=(kc == 0), stop=(kc == 7))
            chk(1.1)
            gz = SM.v(32, 32)
            tt(gz, psg.v(0, 32), BR[l].v(1024, 32), ALU.add)
            v3 = lambda a, lo: a.rearrange("p (a b) -> p a b", b=8)[:, :, lo:lo + 4]
            li = SM.v(32, 32, fn=lambda a: v3(a, 0))
            zf = SM.v(32, 32, fn=lambda a: v3(a, 4))
            c34 = lambda a: a.rearrange("p (a b) -> p a b", b=4)
            sp_ = SM.v(64, 16)
            chk(1.11)
            act(SM.v(64, 16, fn=c34), zf, AF.Exp, scale=-1.0)
            chk(1.12)
            sp0 = sp_
            sp_ = SM.v(144, 16)
            act(sp_, sp0, AF.Ln, bias=B_ONE)
            chk(1.2)
            psb = psum()
            mm(psb.v(0, 16), maskU, sp_)
            mm(psb.v(16, 16), ones1, sp_)
            aP = SM.v(80, 16)
            beta = SM.v(96, 16)
            aL = SM.v(112, 16)
            act(aP, psb.v(0, 16), AF.Exp, scale=-1.0, bias=B_LNS)
            tt(SM.v(128, 16, fn=c34), li, psb.v(0, 16, fn=c34), ALU.add)
            act(beta, SM.v(128, 16), AF.Exp)
            act(aL, psb.v(16, 16), AF.Exp, scale=-1.0)
            chk(1.3)
            w = wload("win", l, 1024, 512)
            for tb in range(4):
                ps = psum()
                for kc in range(8):
                    mm(ps.v(0, 512), XB.v(kc * T + tb * 128, 128), w.v(kc * 512, 512), start=(kc == 0), stop=(kc == 7))
                for h in range(4):
                    act(mb(VA_O + (tb * 4 + h) * 129, 128), ps.v(h * 128, 128), AF.Identity, scale=SM.v(96 + tb * 4 + h, 1))
            copy(mb(VA_O, 16 * 129, fn=lambda a: a.rearrange("p (a b) -> p a b", b=129)[:, :, 128:129]),
                 SM.v(96, 16, fn=lambda a: a.rearrange("p (a b) -> p a b", b=1)))
            chk(1.4)
            for g in range(2):
                w = wload("win", l, g * 512, 512)
                for j in range(4):
                    c = g * 4 + j
                    pre = tm()

                    def ev(psv, pre=pre, c=c):
                        act(pre.v(4, T), psv, AF.Copy)
                    proj_fm(w, 512, j, 8, xbk, ev)
                    copy(pre.v(1, 3), HALM[l].v(c * 3, 3))
                    acc = tm().v(0, T)
                    ts(acc, pre.v(4, T), cv.v(c * 4 + 3, 1), cv.v(32 + c, 1), ALU.mult, ALU.add)
                    for jj in (2, 1, 0):
                        stt(acc, pre.v(1 + jj, T), cv.v(c * 4 + jj, 1), acc, ALU.mult, ALU.add)
                    copy(HALM[l].v(c * 3, 3), pre.v(513, 3))
                    act(mb((QT_O if g == 0 else KT_O) + j * 512, 512), acc, AF.Silu)
            chk(1.5)
            for tb in range(4):
                for h in range(4):
                    qb = mb(QT_O + h * 512 + tb * 128, 128)
                    kb = mb(KT_O + h * 512 + tb * 128, 128)
                    va = mb(VA_O + (tb * 4 + h) * 129, 129)
                    col = tb * 4 + h
                    psA = psum()
                    mm(psA.v(0, 128), kb, qb)
                    sts = SMB.v(((tb * 4 + h) % 2) * 128, 128)
                    tt(sts, psA.v(0, 128), maskU, ALU.mult)
                    pT = psumT()
                    tr(pT.v(0, 128), kb, identB)
                    ktok = SMB.v(256 + ((tb * 4 + h) % 2) * 128, 128)
                    act(ktok, pT.v(0, 128), AF.Copy)
                    psB = psum()
                    mm(psB.v(0, 129), sts, va, start=True, stop=False)
                    mm(psB.v(0, 129), qb, CSB[l].v(h * 129, 129), start=False, stop=True)
                    d1 = SM.v(160, 1)
                    d2 = SM.v(161, 1)
                    rr_ = SM.v(162, 1)
                    act(d1, psB.v(128, 1), AF.Abs, scale=SM.v(80 + col, 1))
                    ts(d1, d1, 1.0, None, ALU.max)
                    recip(d2, d1)
                    tt(rr_, d2, SM.v(80 + col, 1), ALU.mult)
                    hm = tm().v(0, 128)
                    stt(hm, psB.v(0, 128), rr_, OG(tb, h), ALU.mult, ALU.mult)
                    sq = tm().v(0, 128)
                    act(sq, hm, AF.Square)
                    ss = SM.v(163, 1)
                    rsum(ss, sq)
                    act(ss, ss, AF.Sqrt, scale=1.0 / 128, bias=B_NEPS)
                    rs = SM.v(164, 1)
                    recip(rs, ss)
                    stt(mb(YMK_O + tb * 512 + h * 128, 128), hm, rs, BR[l].v(h * 128, 128), ALU.mult, ALU.mult)
                    psC = psum()
                    mm(psC.v(0, 129), ktok, va)
                    ca = tm().v(0, 129)
                    ts(ca, CS[l].v(h * 129, 129), SM.v(112 + col, 1), None, ALU.mult)
                    stt(CS[l].v(h * 129, 129), psC.v(0, 129), SM.v(112 + col, 1), ca, ALU.mult, ALU.add)
                    act(CSB[l].v(h * 129, 129), CS[l].v(h * 129, 129), AF.Copy)
                for c in range(4):
                    pT = psumT()
                    tr(pT.v(0, 128), mb(YMK_O + tb * 512 + c * 128, 128), identB)
                    act(YM.v(c * 512 + tb * 128, 128), pT.v(0, 128), AF.Copy)

            chk(2)
            whq = wload("win", l, 2056, 512)
            whf = wload("win", l, 2568, 512)
            for h in range(4):
                sq_ = tm().v(0, T)
                proj_fm(whq, 512, h, 8, xbk, lambda psv, sq_=sq_: act(sq_, psv, AF.Silu))
                f_ = tm().v(0, T)
                proj_fm(whf, 512, h, 8, xbk, lambda psv, f_=f_: act(f_, psv, AF.Sigmoid))
                ts(f_, f_, LC[l].v(4 + h, 1), LC[l].v(h, 1), ALU.mult, ALU.add)
                lf_ = tm().v(0, T)
                act(lf_, f_, AF.Ln)
                Bt = tm()
                memset(Bt.v(0, 1), 0.0)
                scan(Bt.v(1, T), ONES.v(0, T), lf_, 0.0, ALU.mult, ALU.add)
                nB = tm()
                ts(nB.v(0, 513), Bt.v(0, 513), -1.0, None, ALU.mult)
                Eq = tm().v(0, T)
                Ek = lf_
                for c in range(8):
                    act(V(Eq.t, Eq.name, c * 64, c * 64 + 64), Bt.v(1 + c * 64, 64), AF.Exp, bias=nB.v(64 * c + 64, 1))
                    act(V(Ek.t, Ek.name, c * 64, c * 64 + 64), nB.v(1 + c * 64, 64), AF.Exp, bias=Bt.v(64 * c + 64, 1))
                dd = SM.v(176 + h * 8, 8)
                tt(dd, Bt.v(64, 449, fn=lambda a: a[:, ::64]), Bt.v(0, 449, fn=lambda a: a[:, ::64]), ALU.subtract)
                act(dd, dd, AF.Exp)
                tt(mb(QH_O + h * 512, 512), sq_, Eq, ALU.mult)
                ts(f_, f_, -1.0, 1.0, ALU.mult, ALU.add)
                tt(mb(KH_O + h * 512, 512), f_, Ek, ALU.mult)
            w = wload("win", l, 3080, 512)
            for c in range(8):
                ps = psum()
                for kc in range(8):
                    mm(ps.v(0, 512, p1=64), XB.v(kc * T + c * 64, 64), w.v(kc * 512, 512), start=(kc == 0), stop=(kc == 7))
                act(mb(VH_O + c * 512, 512, p1=64), ps.v(0, 512, p1=64), AF.Copy)
            w = wload("win", l, 3592, 512)
            for j in range(4):
                proj_fm(w, 512, j, 8, xbk, lambda psv, j=j: act(mb(SGH_O + j * 512, 512), psv, AF.Sigmoid))
            for c in range(8):
                for h in range(4):
                    qb = mb(QH_O + h * 512 + c * 64, 64)
                    kb = mb(KH_O + h * 512 + c * 64, 64)
                    vv = mb(VH_O + c * 512 + h * 128, 128, p1=64)
                    par = (c * 4 + h) % 2
                    psA = psum()
                    mm(psA.v(0, 64, p1=64), kb, qb)
                    asb = SMB.v(512 + par * 64, 64, p1=64)
                    tt(asb, psA.v(0, 64, p1=64), CST.v(0, 64, p1=64), ALU.mult)
                    pT = psumT()
                    tr(pT.v(0, 128, p1=64), kb, identB)
                    ktok = SMB.v(640 + par * 128, 128, p1=64)
                    act(ktok, pT.v(0, 128, p1=64), AF.Copy)
                    dcol = SM.v(176 + h * 8 + c, 1)
                    sdb = SMB.v(896, 128)
                    ts(sdb, HS[l].v(h * 128, 128), dcol, None, ALU.mult)
                    ts(HS[l].v(h * 128, 128), HS[l].v(h * 128, 128), dcol, None, ALU.mult)
                    psO = psum()
                    mm(psO.v(0, 128, p1=64), asb, vv, start=True, stop=False)
                    mm(psO.v(0, 128, p1=64), qb, sdb, start=False, stop=True)
                    psS = psum()
                    mm(psS.v(0, 128), ktok, vv)
                    tt(HS[l].v(h * 128, 128), HS[l].v(h * 128, 128), psS.v(0, 128), ALU.add)
                    sq = tm().v(0, 128, p1=64)
                    act(sq, psO.v(0, 128, p1=64), AF.Square)
                    ss = SM.v(165, 1, p1=64)
                    rsum(ss, sq)
                    act(ss, ss, AF.Sqrt, scale=1.0 / 128, bias=CB.v(3, 1, p1=64))
                    rs = SM.v(166, 1, p1=64)
                    recip(rs, ss)
                    stt(YHK.v(c * 512 + h * 128, 128, p1=64), psO.v(0, 128, p1=64), rs, BR[l].v(512 + h * 128, 128, p1=64), ALU.mult, ALU.mult)
                for j in range(4):
                    pT = psumT()
                    tr(pT.v(0, 64), YHK.v(c * 512 + j * 128, 128, p1=64), IDB.v(0, 64, p1=64))
                    tt(YH.v(j * 512 + c * 64, 64), pT.v(0, 64), mb(SGH_O + j * 512 + c * 64, 64), ALU.mult)

            chk(3)
            wrx = wload("win", l, 4104, 512)
            wrg = wload("win", l, 4616, 512)
            for c in range(4):
                pre = tm()
                proj_fm(wrx, 512, c, 8, xbk, lambda psv, pre=pre: act(pre.v(4, T), psv, AF.Copy))
                copy(pre.v(1, 3), HALR[l].v(c * 3, 3))
                u = tm().v(0, T)
                ts(u, pre.v(4, T), cv.v(40 + c * 4 + 3, 1), cv.v(56 + c, 1), ALU.mult, ALU.add)
                for jj in (2, 1, 0):
                    stt(u, pre.v(1 + jj, T), cv.v(40 + c * 4 + jj, 1), u, ALU.mult, ALU.add)
                copy(HALR[l].v(c * 3, 3), pre.v(513, 3))
                ub = SMB.v(0, 512)
                act(ub, u, AF.Copy)
                psr = psum()
                mm(psr.v(0, T), RW[l].v(c * 128, 128), ub)
                psi = psum()
                mm(psi.v(0, T), RW[l].v(512 + c * 128, 128), ub)
                r_ = tm().v(0, T)
                i_ = tm().v(0, T)
                act(r_, psr.v(0, T), AF.Sigmoid, bias=cv.v(60 + c, 1))
                act(i_, psi.v(0, T), AF.Sigmoid, bias=cv.v(64 + c, 1))
                a_ = tm().v(0, T)
                th = tm().v(0, T)
                act(a_, r_, AF.Exp, scale=LC[l].v(8 + c, 1))
                act(th, r_, AF.Tanh, scale=LC[l].v(12 + c, 1))
                ts(r_, th, 1.0, None, ALU.add)
                recip(r_, r_)
                tt(th, th, r_, ALU.mult)
                act(th, th, AF.Sqrt, scale=2.0)
                tt(i_, i_, u, ALU.mult)
                tt(i_, i_, th, ALU.mult)
                hh = tm().v(0, T)
                scan(hh, a_, i_, RST[l].v(c, 1), ALU.mult, ALU.add)
                copy(RST[l].v(c, 1), V(hh.t, hh.name, T - 1, T))
                ge = u
                proj_fm(wrg, 512, c, 8, xbk, lambda psv, ge=ge: act(ge, psv, AF.Gelu))
                tt(YR.v(c * 512, 512), hh, ge, ALU.mult)

            chk(4)
            for bi, (bn, Y) in enumerate((("wbm", YM), ("wbh", YH), ("wbr", YR))):
                wb_ = wload(bn, l, 0, 1024)
                for half in range(2):
                    wgt = wload("win", l, 5128 + bi * 1024 + half * 512, 512)
                    for jj in range(4):
                        j = half * 4 + jj
                        sg = tm().v(0, T)
                        proj_fm(wgt, 512, jj, 8, xbk, lambda psv, sg=sg: act(sg, psv, AF.Sigmoid))
                        psP = psum()
                        for kc in range(4):
                            mm(psP.v(0, T), wb_.v(kc * 1024 + j * 128, 128), Y.v(kc * 512, 512), start=(kc == 0), stop=(kc == 3))
                        if bi == 0:
                            tt(MIX.v(j * T, T), sg, psP.v(0, T), ALU.mult)
                        else:
                            tt(sg, sg, psP.v(0, T), ALU.mult)
                            tt(MIX.v(j * T, T), MIX.v(j * T, T), sg, ALU.add)
                        if bi == 2:
                            act(MIXB.v(j * T, T), MIX.v(j * T, T), AF.Copy)
            for half in range(2):
                wo = wload("wout", l, half * 512, 512)
                for jj in range(4):
                    j = half * 4 + jj
                    proj_fm(wo, 512, jj, 8, lambda kc: MIXB.v(kc * T, T),
                            lambda psv, j=j: stt(XT.v(j * T, T), XT.v(j * T, T), ALPHA, psv, ALU.mult, ALU.add))
            layer_norm(l, 72, 80)

            chk(5)
            for g in range(6):
                n = 512 if g < 5 else 256
                wg_ = wload("wg", l, g * 512, n)
                wu_ = wload("wu", l, g * 512, n)
                for jj in range(n // 128):
                    j = g * 4 + jj
                    sg = tm().v(0, T)
                    proj_fm(wg_, n, jj, 8, xbk, lambda psv, sg=sg: act(sg, psv, AF.Silu))
                    proj_fm(wu_, n, jj, 8, xbk, lambda psv, sg=sg, j=j: tt(mb(j * T, T), sg, psv, ALU.mult))
            for j in range(8):
                wd_ = wload("wd", l, j * 128, 128)
                proj_fm(wd_, 128, 0, 22, lambda kc: mb(kc * T, T),
                        lambda psv, j=j: stt(XT.v(j * T, T), XT.v(j * T, T), ALPHA, psv, ALU.mult, ALU.add))
            layer_norm(l, 88, 96)
          except _Stop:
            pass

        osrc = V(XT.t, XT.name, 0, 8 * T, fn=lambda a: a.rearrange("p (k n) -> p k n", k=8))
        S.dma("pool", (lambda e, osrc=osrc, t0=t0: e.dma_start(
            out=outT_d.rearrange("(k p) s -> p k s", p=128)[:, :, t0:t0 + T], in_=osrc.ap)),
            reads=[osrc], writes=[], semkey=f"o{ti % 2}")

    S.wait_all("pool", [(k, v) for k, v in S.dmacnt.items() if k.startswith("o")])

    EP = 16000
    sems = {}
    for k in sorted(S.semkeys):
        if k in Sched.ENG:
            for ep in range(S.cnt[k] // EP + 1):
                sems[(k, ep)] = es.enter_context(nc.semaphore(f"sem_{k}_{ep}"))
        else:
            sems[k] = es.enter_context(nc.semaphore(f"sem_{k}"))
    engmap = {"pe": "tensor", "act": "scalar", "dve": "vector", "pool": "gpsimd", "sp": "sync"}
    with nc.Block() as block:
        for ename, bname in engmap.items():
            def body(e, ename=ename):
                for waits, fn, inc in S.ops[ename]:
                    for k, v in waits:
                        if k in Sched.ENG:
                            e.wait_ge(sems[(k, (v - 1) // EP)], (v - 1) % EP + 1)
                        else:
                            e.wait_ge(sems[k], v)
                    if fn is not None:
                        ins = fn(e)
                        if inc is not None:
                            if inc[0] in Sched.ENG:
                                ins.then_inc(sems[(inc[0], (inc[1] - 1) // EP)], 1)
                            else:
                                ins.then_inc(sems[inc[0]], inc[1])
            getattr(block, bname)(body)
    es.close()
    stats = {k: len(v) for k, v in S.ops.items()}
    return nc, stats


def _consts():
    c = np.zeros((128, 512), np.float32)
    c[:, 0:128] = np.triu(np.ones((128, 128), np.float32))
    c[:, 128:256] = np.eye(128, dtype=np.float32)
    c[:, 256:384] = 1.0 / 1024.0
    c[:, 384:512] = 1.0
    return c


def _pack(inp, n_layers):
    f = lambda a: np.asarray(a, np.float32)
    cvs, brs, rws = [], [], []
    hlb = f(inp["h_lower_bounds"])
    for l in range(n_layers):
        cv = np.zeros((128, NCV), np.float32)
        cv[:, 0:32] = f(inp["m_conv_w"])[l].reshape(4, 8, 128).transpose(2, 1, 0).reshape(128, 32)
        cv[:, 32:40] = f(inp["m_conv_b"])[l].reshape(8, 128).T
        cv[:, 40:56] = f(inp["r_conv_w"])[l].reshape(4, 4, 128).transpose(2, 1, 0).reshape(128, 16)
        cv[:, 56:60] = f(inp["r_conv_b"])[l].reshape(4, 128).T
        cv[:, 60:64] = f(inp["r_b_rec"])[l].reshape(4, 128).T
        cv[:, 64:68] = f(inp["r_b_in"])[l].reshape(4, 128).T
        cv[:, 68:72] = f(inp["r_lambda"])[l].reshape(4, 128).T
        cv[:, 72:80] = f(inp["ln1_g"])[l].reshape(8, 128).T
        cv[:, 80:88] = f(inp["ln1_b"])[l].reshape(8, 128).T
        cv[:, 88:96] = f(inp["ln2_g"])[l].reshape(8, 128).T
        cv[:, 96:104] = f(inp["ln2_b"])[l].reshape(8, 128).T
        cv[:, 104:108] = hlb[0].reshape(4, 128).T
        cv[:, 108:112] = hlb[min(1, hlb.shape[0] - 1)].reshape(4, 128).T
        cvs.append(cv)
        br = np.zeros((128, 1056), np.float32)
        br[:, 0:512] = f(inp["m_norm_g"])[l][None, :]
        br[:, 512:1024] = f(inp["h_norm_g"])[l][None, :]
        bif = np.concatenate([f(inp["m_bias_i"])[l], f(inp["m_bias_f"])[l]])
        br[:, 1024:1056] = np.tile(bif, 4)[None, :]
        brs.append(br)
        rw = np.concatenate([f(inp["r_w_rec"])[l].transpose(1, 0, 2).reshape(128, 512),
                             f(inp["r_w_in"])[l].transpose(1, 0, 2).reshape(128, 512)], axis=1)
        rws.append(rw)
    return np.stack(cvs), np.stack(brs), np.stack(rws)


_CACHE = {}


def run(inputs, S_len, n_layers, seqs, n_cores):
    key = (S_len, n_layers)
    if key not in _CACHE:
        _CACHE[key] = build_program(S_len, n_layers)[0]
    nc = _CACHE[key]
    cv, br, rw = _pack(inputs, n_layers)
    cm = _consts()
    shared = {
        "w_in": np.ascontiguousarray(inputs["w_in"][:n_layers], np.float32),
        "w_branch_m": np.ascontiguousarray(inputs["w_branch_m"][:n_layers], np.float32),
        "w_branch_h": np.ascontiguousarray(inputs["w_branch_h"][:n_layers], np.float32),
        "w_branch_r": np.ascontiguousarray(inputs["w_branch_r"][:n_layers], np.float32),
        "w_out": np.ascontiguousarray(inputs["w_out"][:n_layers], np.float32),
        "w_ff_gate": np.ascontiguousarray(inputs["w_ff_gate"][:n_layers], np.float32),
        "w_ff_up": np.ascontiguousarray(inputs["w_ff_up"][:n_layers], np.float32),
        "w_ff_down": np.ascontiguousarray(inputs["w_ff_down"][:n_layers], np.float32),
        "cvec": cv, "brow": br, "rw": rw, "cmat": cm,
    }
    in_maps = []
    for c in range(n_cores):
        m = dict(shared)
        m["xT"] = np.ascontiguousarray(seqs[c % len(seqs)].T)
        in_maps.append(m)
    res = run_bass_kernel_spmd(nc, in_maps, core_ids=list(range(n_cores)))
    return [np.ascontiguousarray(r["outT"].T) for r in res.results]


def kernel(**inputs):
    x = np.asarray(inputs["x"], np.float32)
    B, S_len, _ = x.shape
    outs = run(inputs, S_len, 2, [x[b] for b in range(B)], 8)
    return np.stack(outs[:B]).astype(np.float32)
```

```python
import math
from contextlib import ExitStack
import numpy as np
import concourse.bass as bass
import concourse.mybir as mybir
from concourse.bass_utils import run_bass_kernel_spmd

F32, BF16 = mybir.dt.float32, mybir.dt.bfloat16
AF = mybir.ActivationFunctionType
ALU = mybir.AluOpType
AX = mybir.AxisListType

D = 1024
DIN = 8200
DFF = 2816
T = 512
ALPHA = 4.0 ** 0.25
LN_EPS = 1e-5
NORM_EPS = 1e-6
NW = 4
WSLOT = 4160
NCV = 112
SAME_ENG_WIN = 8


class V:
    __slots__ = ("t", "name", "lo", "hi", "p0", "p1", "fn")

    def __init__(s, t, name, lo, hi, p0=0, p1=128, fn=None):
        s.t, s.name, s.lo, s.hi, s.p0, s.p1, s.fn = t, name, lo, hi, p0, p1, fn

    @property
    def ap(s):
        a = s.t[s.p0:s.p1, s.lo:s.hi]
        if s.fn is not None:
            a = s.fn(a)
        return a


class Buf:
    def __init__(s, t, name):
        s.t, s.name = t, name

    def v(s, lo, n, p0=0, p1=128, fn=None):
        return V(s.t, s.name, lo, lo + n, p0, p1, fn)


class Sched:
    ENG = ("pe", "act", "dve", "pool", "sp")

    def __init__(s):
        s.ops = {e: [] for e in s.ENG}
        s.cnt = {e: 0 for e in s.ENG}
        s.waited = {e: {} for e in s.ENG}
        s.reg = {}
        s.dmacnt = {}
        s.semkeys = set(s.ENG)

    def _collect(s, need, eng, tok):
        if tok is None:
            return
        k, v = tok
        if k == eng:
            if eng == "pe" or v > s.cnt[eng] or v < s.cnt[eng] - SAME_ENG_WIN:
                return
        if s.waited[eng].get(k, 0) >= v:
            return
        if need.get(k, 0) < v:
            need[k] = v

    def _deps(s, eng, reads, writes, tok):
        need = {}
        for r in reads:
            for ent in s.reg.get(r.name, ()):
                if ent[0] < r.hi and r.lo < ent[1]:
                    s._collect(need, eng, ent[2])
        for w in writes:
            for ent in s.reg.get(w.name, ()):
                if ent[0] < w.hi and w.lo < ent[1]:
                    s._collect(need, eng, ent[2])
                    for k, v in ent[3].items():
                        s._collect(need, eng, (k, v))
        for k, v in need.items():
            s.waited[eng][k] = v
        return sorted(need.items())

    def _record(s, reads, writes, tok):
        for r in reads:
            lst = s.reg.setdefault(r.name, [])
            hit = False
            for ent in lst:
                if ent[0] < r.hi and r.lo < ent[1]:
                    if ent[3].get(tok[0], 0) < tok[1]:
                        ent[3][tok[0]] = tok[1]
                    if ent[0] <= r.lo and r.hi <= ent[1]:
                        hit = True
            if not hit:
                lst.append([r.lo, r.hi, None, {tok[0]: tok[1]}])
        for w in writes:
            lst = s.reg.setdefault(w.name, [])
            lst[:] = [ent for ent in lst if not (w.lo <= ent[0] and ent[1] <= w.hi)]
            lst.append([w.lo, w.hi, tok, {}])

    @staticmethod
    def _bank(reads, writes):
        r2, w2 = [], []
        for r in reads:
            if r.name[0] == "P":
                w2.append(V(None, r.name, 0, 1 << 30))
            else:
                r2.append(r)
        for w in writes:
            w2.append(V(None, w.name, 0, 1 << 30) if w.name[0] == "P" else w)
        return r2, w2

    def op(s, eng, fn, reads=(), writes=(), sig=True):
        reads, writes = s._bank(reads, writes)
        waits = s._deps(eng, reads, writes, None)
        if sig:
            s.cnt[eng] += 1
            tok = (eng, s.cnt[eng])
            inc = (eng, s.cnt[eng])
        else:
            tok = (eng, s.cnt[eng] + 1)
            inc = None
        s._record(reads, writes, tok)
        s.ops[eng].append((waits, fn, inc))

    def dma(s, eng, fn, reads, writes, semkey):
        waits = s._deps(eng, reads, writes, None)
        s.semkeys.add(semkey)
        s.dmacnt[semkey] = s.dmacnt.get(semkey, 0) + 16
        tok = (semkey, s.dmacnt[semkey])
        s._record(reads, writes, tok)
        s.ops[eng].append((waits, fn, (semkey, 16)))

    def wait_all(s, eng, toks):
        s.ops[eng].append((sorted(toks), None, None))


class _Stop(Exception):
    pass


def build_program(S_len, n_layers=2, stage=99):
    assert S_len % T == 0
    ntiles = S_len // T
    nc = bass.Bass("TRN2", target_bir_lowering=False)
    es = ExitStack()
    S = Sched()

    def dram(name, shape, dt, kind):
        return nc.dram_tensor(name, list(shape), dt, kind=kind).ap()

    xT_d = dram("xT", [D, S_len], F32, "ExternalInput")
    outT_d = dram("outT", [D, S_len], F32, "ExternalOutput")
    w_in_d = dram("w_in", [n_layers, D, DIN], F32, "ExternalInput")
    wbm_d = dram("w_branch_m", [n_layers, 512, D], F32, "ExternalInput")
    wbh_d = dram("w_branch_h", [n_layers, 512, D], F32, "ExternalInput")
    wbr_d = dram("w_branch_r", [n_layers, 512, D], F32, "ExternalInput")
    wout_d = dram("w_out", [n_layers, D, D], F32, "ExternalInput")
    wg_d = dram("w_ff_gate", [n_layers, D, DFF], F32, "ExternalInput")
    wu_d = dram("w_ff_up", [n_layers, D, DFF], F32, "ExternalInput")
    wd_d = dram("w_ff_down", [n_layers, DFF, D], F32, "ExternalInput")
    cvec_d = dram("cvec", [n_layers, 128, NCV], F32, "ExternalInput")
    brow_d = dram("brow", [n_layers, 128, 1056], F32, "ExternalInput")
    rw_d = dram("rw", [n_layers, 128, 1024], F32, "ExternalInput")
    cmat_d = dram("cmat", [128, 512], F32, "ExternalInput")

    wspec = [("win", w_in_d, 8, DIN), ("wbm", wbm_d, 4, D), ("wbh", wbh_d, 4, D), ("wbr", wbr_d, 4, D),
             ("wout", wout_d, 8, D), ("wg", wg_d, 8, DFF), ("wu", wu_d, 8, DFF), ("wd", wd_d, 22, D)]
    scr = {}
    for l in range(n_layers):
        for (nm, src, k, n) in wspec:
            t = nc.dram_tensor(f"s_{nm}{l}", [128, k * n], BF16, kind="Internal").ap()
            scr[(nm, l)] = (t, f"s_{nm}{l}", k, n, src)

    def sb(name, n, dt=F32):
        t = es.enter_context(nc.sbuf_tensor(name, [128, n], dt))
        return Buf(t, name)

    XT = sb("XT", 8 * T)
    XB = sb("XB", 8 * T, BF16)
    WB = [sb(f"WB{i}", WSLOT, BF16) for i in range(NW)]
    NTM = 12
    TM = [sb(f"TM{i}", 516) for i in range(NTM)]
    MIX = sb("MIX", 8 * T)
    MB = sb("MBF", 22 * T, BF16)
    YM = sb("YMT", 4 * T, BF16)
    YH = sb("YHT", 4 * T, BF16)
    YR = sb("YRT", 4 * T, BF16)
    MIXB = sb("MIXB", 8 * T, BF16)
    CST = sb("CST", 512)
    IDB = sb("IDB", 128, BF16)
    ONES = sb("ONES", 512)
    CV = [sb(f"CV{l}", NCV) for l in range(n_layers)]
    BR = [sb(f"BR{l}", 1056) for l in range(n_layers)]
    RW = [sb(f"RW{l}", 1024, BF16) for l in range(n_layers)]
    LC = [sb(f"LC{l}", 32) for l in range(n_layers)]
    CS = [sb(f"CS{l}", 4 * 129) for l in range(n_layers)]
    CSB = [sb(f"CSB{l}", 4 * 129, BF16) for l in range(n_layers)]
    HS = [sb(f"HS{l}", 4 * 128) for l in range(n_layers)]
    HALM = [sb(f"HALM{l}", 8 * 3) for l in range(n_layers)]
    HALR = [sb(f"HALR{l}", 4 * 3) for l in range(n_layers)]
    RST = [sb(f"RST{l}", 4) for l in range(n_layers)]
    SM = sb("SM", 512)
    SMB = sb("SMB", 1024, BF16)
    AS4 = sb("AS4", 512, BF16)
    KT4 = sb("KT4", 1024, BF16)
    HSB = [sb(f"HSB{l}", 512, BF16) for l in range(n_layers)]
    RM = sb("RM", 512)
    MK4 = sb("MK4", 256)

    PS = [Buf(es.enter_context(nc.psum_tensor(f"PS{i}", [128, 512], F32)), f"PS{i}") for i in range(6)]
    PT = [Buf(es.enter_context(nc.psum_tensor(f"PT{i}", [128, 1024], BF16)), f"PT{i}") for i in range(2)]
    rr = {"ps": 0, "pt": 0, "w": 0, "tm": 0}

    def psum():
        rr["ps"] = (rr["ps"] + 1) % 6
        return PS[rr["ps"]]

    def psumT():
        rr["pt"] = (rr["pt"] + 1) % 2
        return PT[rr["pt"]]

    def mb(lo, n, **kw):
        return MB.v(lo, n, **kw)
    QT_O, KT_O, VA_O, OG_O, YMK_O = 0, 2048, 4096, 4096 + 2064, 4096 + 2064 + 2048
    QH_O, KH_O, VH_O, SGH_O, YHK_O = 0, 2048, 4096, 8192, 10240
    YHK = sb("YHK", 8 * T, BF16)

    maskU = CST.v(0, 128)
    identF = CST.v(128, 128)
    onesM = CST.v(256, 128)
    ones1 = CST.v(384, 128)

    def mm(out, lhsT, rhs, start=True, stop=True):
        S.op("pe", lambda e: e.matmul(out.ap, lhsT=lhsT.ap, rhs=rhs.ap, start=start, stop=stop),
             reads=[lhsT, rhs], writes=[out], sig=stop)

    def tr(out, in_, ident):
        S.op("pe", lambda e: e.transpose(out.ap, in_.ap, ident.ap), reads=[in_, ident], writes=[out])

    def act(out, in_, func, bias=None, scale=None):
        rd = [in_]
        kw = {}
        if bias is not None:
            if isinstance(bias, V):
                rd.append(bias)
            kw["bias"] = bias
        if scale is not None:
            if isinstance(scale, V):
                rd.append(scale)
            kw["scale"] = scale

        def f(e):
            k2 = {k: (v.ap if isinstance(v, V) else v) for k, v in kw.items()}
            return e.activation(out=out.ap, in_=in_.ap, func=func, **k2)
        S.op("act", f, reads=rd, writes=[out])

    def ts(out, in0, s1, s2, op0, op1=None, eng="dve"):
        rd = [in0] + [x for x in (s1, s2) if isinstance(x, V)]

        def f(e):
            a1 = s1.ap if isinstance(s1, V) else s1
            a2 = s2.ap if isinstance(s2, V) else s2
            if op1 is None:
                return e.tensor_scalar(out=out.ap, in0=in0.ap, scalar1=a1, scalar2=None, op0=op0)
            return e.tensor_scalar(out=out.ap, in0=in0.ap, scalar1=a1, scalar2=a2, op0=op0, op1=op1)
        S.op(eng, f, reads=rd, writes=[out])

    def stt(out, in0, scalar, in1, op0, op1):
        rd = [in0, in1] + ([scalar] if isinstance(scalar, V) else [])

        def f(e):
            sc = scalar.ap if isinstance(scalar, V) else scalar
            return e.scalar_tensor_tensor(out=out.ap, in0=in0.ap, scalar=sc, in1=in1.ap, op0=op0, op1=op1)
        S.op("dve", f, reads=rd, writes=[out])

    def tt(out, in0, in1, op, eng="dve"):
        S.op(eng, lambda e: e.tensor_tensor(out=out.ap, in0=in0.ap, in1=in1.ap, op=op), reads=[in0, in1], writes=[out])

    def scan(out, d0, d1, init, op0, op1):
        rd = [d0, d1] + ([init] if isinstance(init, V) else [])

        def f(e):
            i = init.ap if isinstance(init, V) else init
            return e.tensor_tensor_scan(out=out.ap, data0=d0.ap, data1=d1.ap, initial=i, op0=op0, op1=op1)
        S.op("dve", f, reads=rd, writes=[out])

    def recip(out, in_):
        S.op("dve", lambda e: e.reciprocal(out=out.ap, in_=in_.ap), reads=[in_], writes=[out])

    def copy(out, in_, eng="dve"):
        S.op(eng, lambda e: e.tensor_copy(out=out.ap, in_=in_.ap), reads=[in_], writes=[out])

    def rsum(out, in_):
        S.op("dve", lambda e: e.reduce_sum(out=out.ap, in_=in_.ap, axis=AX.X), reads=[in_], writes=[out])

    def memset(buf_v, val, eng="dve"):
        S.op(eng, lambda e: e.memset(buf_v.ap, val), reads=[], writes=[buf_v])

    def dma(eng, out, in_, semkey):
        S.dma(eng, lambda e: e.dma_start(out=out.ap, in_=in_.ap), reads=[in_], writes=[out], semkey=semkey)

    def dview(ap, name, n, fn=None):
        return V(ap, name, 0, n, fn=fn)

    for l in range(n_layers):
        if stage < 0:
            break
        for (nm, src, k, n) in wspec:
            t, tname, k, n, src = scr[(nm, l)]
            semkey = f"cast_{nm}{l}"
            for kc in range(k):
                for c0 in range(0, n, 2048):
                    c1 = min(n, c0 + 2048)
                    sap = src[l][kc * 128:(kc + 1) * 128, c0:c1]
                    dst = V(t, tname, kc * n + c0, kc * n + c1)
                    S.dma("pool", (lambda e, dst=dst, sap=sap: e.dma_start(out=dst.ap, in_=sap)), reads=[], writes=[dst],
                          semkey=semkey)
            S.reg[tname] = [[0, k * n, (semkey, S.dmacnt[semkey]), {}]]
    S.dma("sp", lambda e: e.dma_start(out=CST.t[:, :], in_=cmat_d), reads=[], writes=[CST.v(0, 512)], semkey="c0")
    for l in range(n_layers):
        S.dma("sp", (lambda e, l=l: e.dma_start(out=CV[l].t[:, :], in_=cvec_d[l])), reads=[], writes=[CV[l].v(0, NCV)], semkey=f"c1_{l}")
        S.dma("sp", (lambda e, l=l: e.dma_start(out=BR[l].t[:, :], in_=brow_d[l])), reads=[], writes=[BR[l].v(0, 1056)], semkey=f"c2_{l}")
        S.dma("pool", (lambda e, l=l: e.dma_start(out=RW[l].t[:, :], in_=rw_d[l])), reads=[], writes=[RW[l].v(0, 1024)], semkey=f"c3_{l}")
    S.dma("pool", lambda e: e.dma_start(out=IDB.t[:, :], in_=cmat_d[:, 128:256]), reads=[], writes=[IDB.v(0, 128)], semkey="c4")
    identB = IDB.v(0, 128)
    CB = sb("CB", 8)
    memset(CB.v(0, 1), 1.0)
    memset(CB.v(1, 1), math.log(128.0 ** -0.5))
    memset(CB.v(2, 1), LN_EPS)
    memset(CB.v(3, 1), NORM_EPS)
    B_ONE, B_LNS, B_LNEPS, B_NEPS = CB.v(0, 1), CB.v(1, 1), CB.v(2, 1), CB.v(3, 1)
    memset(ONES.v(0, 512), 1.0)
    memset(RM.v(0, 512), 1.0)
    memset(RM.v(0, 449, fn=lambda a: a[:, ::64]), 0.0)
    for h in range(4):
        copy(MK4.v(h * 64, 64, p1=64), CST.v(0, 64, p1=64))
    for l in range(n_layers):
        memset(CS[l].v(0, 516), 0.0)
        memset(CSB[l].v(0, 516), 0.0)
        memset(HS[l].v(0, 512), 0.0)
        memset(HSB[l].v(0, 512), 0.0)
        memset(HALM[l].v(0, 24), 0.0)
        memset(HALR[l].v(0, 12), 0.0)
        memset(RST[l].v(0, 4), 0.0)
    e0 = SM.v(0, 4)
    e1 = SM.v(4, 4)
    esum = SM.v(8, 4)
    act(e0, CV[0].v(104, 4), AF.Exp)
    act(e1, CV[0].v(108, 4), AF.Exp)
    tt(esum, e0, e1, ALU.add)
    recip(esum, esum)
    memset(LC[0].v(0, 4), 0.0)
    if n_layers > 1:
        tt(LC[1].v(0, 4), e1, esum, ALU.mult)
    for l in range(n_layers):
        ts(LC[l].v(4, 4), LC[l].v(0, 4), -1.0, 1.0, ALU.mult, ALU.add)
        act(SM.v(12, 4), CV[l].v(68, 4), AF.Exp, scale=-1.0)
        act(SM.v(16, 4), SM.v(12, 4), AF.Ln, bias=B_ONE)
        ts(LC[l].v(8, 4), SM.v(16, 4), -8.0, None, ALU.mult)
        ts(LC[l].v(12, 4), SM.v(16, 4), 8.0, None, ALU.mult)

    def wload(nm, l, c0, n):
        t, tname, k, ntot, _ = scr[(nm, l)]
        rr["w"] = (rr["w"] + 1) % NW
        slot = WB[rr["w"]]
        src = V(t, tname, 0, k * ntot, fn=lambda a: a.rearrange("p (k n) -> p k n", k=k)[:, :, c0:c0 + n])
        dst = V(slot.t, slot.name, 0, k * n, fn=lambda a: a.rearrange("p (k n) -> p k n", k=k))
        dma("sp", dst, src, f"w{rr['w']}")
        return slot

    def tm():
        rr["tm"] = (rr["tm"] + 1) % NTM
        return TM[rr["tm"]]

    LNS = math.log(128.0 ** -0.5)

    def chk(k):
        if stage < k:
            raise _Stop()

    def proj_fm(slot, n, j, nk, rhs_of_k, evac):
        ps = psum()
        for kc in range(nk):
            mm(ps.v(0, T), slot.v(kc * n + j * 128, 128), rhs_of_k(kc), start=(kc == 0), stop=(kc == nk - 1))
        evac(ps.v(0, T))

    xbk = lambda kc: XB.v(kc * T, T)

    def layer_norm(l, goff, boff):
        s1 = psum()
        s2 = psum()
        for j in range(8):
            zq = tm()
            act(zq.v(0, T), XT.v(j * T, T), AF.Square)
            mm(s1.v(0, T), onesM, XT.v(j * T, T), start=(j == 0), stop=(j == 7))
            mm(s2.v(0, T), onesM, zq.v(0, T), start=(j == 0), stop=(j == 7))
        mean = tm().v(0, T)
        var = tm().v(0, T)
        rstd = tm().v(0, T)
        nmr = tm().v(0, T)
        act(mean, s1.v(0, T), AF.Copy)
        tt(var, mean, mean, ALU.mult)
        tt(var, s2.v(0, T), var, ALU.subtract)
        act(var, var, AF.Sqrt, bias=B_LNEPS)
        recip(rstd, var)
        stt(nmr, mean, -1.0, rstd, ALU.mult, ALU.mult)
        for j in range(8):
            t1 = tm().v(0, T)
            g = CV[l].v(goff + j, 1)
            b = CV[l].v(boff + j, 1)
            stt(t1, XT.v(j * T, T), g, rstd, ALU.mult, ALU.mult)
            stt(t1, nmr, g, t1, ALU.mult, ALU.add)
            ts(XT.v(j * T, T), t1, b, None, ALU.add)
            act(XB.v(j * T, T), t1, AF.Identity, bias=b)

    for ti in range(ntiles):
        t0 = ti * T
        xsrc = V(xT_d, "xT_d", 0, 1, fn=lambda a, t0=t0: a.rearrange("(k p) s -> p k s", p=128)[:, :, t0:t0 + T])
        xsrc.fn = xsrc.fn
        xdst = V(XT.t, XT.name, 0, 8 * T, fn=lambda a: a.rearrange("p (k n) -> p k n", k=8))
        S.dma("pool", (lambda e, xdst=xdst, t0=t0: e.dma_start(
            out=xdst.ap, in_=xT_d.rearrange("(k p) s -> p k s", p=128)[:, :, t0:t0 + T])),
            reads=[], writes=[xdst], semkey=f"x{ti % 2}")
        for j in range(8):
            act(XB.v(j * T, T), XT.v(j * T, T), AF.Copy)

        for l in range(n_layers):
          try:
            cv = CV[l]
            chk(1)
            w = wload("win", l, 1536, 520)
            OG = lambda tb, h: mb(OG_O + tb * 512 + h * 128, 128)
            for tb in range(4):
                ps = psum()
                for kc in range(8):
                    mm(ps.v(0, 512), XB.v(kc * T + tb * 128, 128), w.v(kc * 520, 512), start=(kc == 0), stop=(kc == 7))
                act(mb(OG_O + tb * 512, 512), ps.v(0, 512), AF.Sigmoid)
            psg = psum()
            for tb in range(4):
                for kc in range(8):
                    mm(psg.v(tb * 8, 8), XB.v(kc * T + tb * 128, 128), w.v(kc * 520 + 512, 8), start=(kc == 0), stop=(kc == 7))
            chk(1.1)
            gz = SM.v(32, 32)
            tt(gz, psg.v(0, 32), BR[l].v(1024, 32), ALU.add)
            v3 = lambda a, lo: a.rearrange("p (a b) -> p a b", b=8)[:, :, lo:lo + 4]
            li = SM.v(32, 32, fn=lambda a: v3(a, 0))
            zf = SM.v(32, 32, fn=lambda a: v3(a, 4))
            c34 = lambda a: a.rearrange("p (a b) -> p a b", b=4)
            sp_ = SM.v(64, 16)
            chk(1.11)
            act(SM.v(64, 16, fn=c34), zf, AF.Exp, scale=-1.0)
            chk(1.12)
            sp0 = sp_
            sp_ = SM.v(144, 16)
            act(sp_, sp0, AF.Ln, bias=B_ONE)
            chk(1.2)
            psb = psum()
            mm(psb.v(0, 16), maskU, sp_)
            mm(psb.v(16, 16), ones1, sp_)
            aP = SM.v(80, 16)
            beta = SM.v(96, 16)
            aL = SM.v(112, 16)
            act(aP, psb.v(0, 16), AF.Exp, scale=-1.0, bias=B_LNS)
            tt(SM.v(128, 16, fn=c34), li, psb.v(0, 16, fn=c34), ALU.add)
            act(beta, SM.v(128, 16), AF.Exp)
            act(aL, psb.v(16, 16), AF.Exp, scale=-1.0)
            chk(1.3)
            w = wload("win", l, 1024, 512)
            for tb in range(4):
                ps = psum()
                for kc in range(8):
                    mm(ps.v(0, 512), XB.v(kc * T + tb * 128, 128), w.v(kc * 512, 512), start=(kc == 0), stop=(kc == 7))
                for h in range(4):
                    act(mb(VA_O + (tb * 4 + h) * 129, 128), ps.v(h * 128, 128), AF.Identity, scale=SM.v(96 + tb * 4 + h, 1))
            copy(mb(VA_O, 16 * 129, fn=lambda a: a.rearrange("p (a b) -> p a b", b=129)[:, :, 128:129]),
                 SM.v(96, 16, fn=lambda a: a.rearrange("p (a b) -> p a b", b=1)))
            chk(1.4)
            for g in range(2):
                w = wload("win", l, g * 512, 512)
                for j in range(4):
                    c = g * 4 + j
                    pre = tm()

                    def ev(psv, pre=pre, c=c):
                        act(pre.v(4, T), psv, AF.Copy)
                    proj_fm(w, 512, j, 8, xbk, ev)
                    copy(pre.v(1, 3), HALM[l].v(c * 3, 3))
                    acc = tm().v(0, T)
                    ts(acc, pre.v(4, T), cv.v(c * 4 + 3, 1), cv.v(32 + c, 1), ALU.mult, ALU.add)
                    for jj in (2, 1, 0):
                        stt(acc, pre.v(1 + jj, T), cv.v(c * 4 + jj, 1), acc, ALU.mult, ALU.add)
                    copy(HALM[l].v(c * 3, 3), pre.v(513, 3))
                    act(mb((QT_O if g == 0 else KT_O) + j * 512, 512), acc, AF.Silu)
            chk(1.5)
            for tb in range(4):
                for h in range(4):
                    qb = mb(QT_O + h * 512 + tb * 128, 128)
                    kb = mb(KT_O + h * 512 + tb * 128, 128)
                    va = mb(VA_O + (tb * 4 + h) * 129, 129)
                    col = tb * 4 + h
                    psA = psum()
                    mm(psA.v(0, 128), kb, qb)
                    sts = SMB.v(((tb * 4 + h) % 2) * 128, 128)
                    tt(sts, psA.v(0, 128), maskU, ALU.mult)
                    pT = psumT()
                    tr(pT.v(0, 128), kb, identB)
                    ktok = SMB.v(256 + ((tb * 4 + h) % 2) * 128, 128)
                    act(ktok, pT.v(0, 128), AF.Copy)
                    psB = psum()
                    mm(psB.v(0, 129), sts, va, start=True, stop=False)
                    mm(psB.v(0, 129), qb, CSB[l].v(h * 129, 129), start=False, stop=True)
                    d1 = SM.v(160, 1)
                    d2 = SM.v(161, 1)
                    rr_ = SM.v(162, 1)
                    act(d1, psB.v(128, 1), AF.Abs, scale=SM.v(80 + col, 1))
                    ts(d1, d1, 1.0, None, ALU.max)
                    recip(d2, d1)
                    tt(rr_, d2, SM.v(80 + col, 1), ALU.mult)
                    hm = tm().v(0, 128)
                    stt(hm, psB.v(0, 128), rr_, OG(tb, h), ALU.mult, ALU.mult)
                    sq = tm().v(0, 128)
                    act(sq, hm, AF.Square)
                    ss = SM.v(163, 1)
                    rsum(ss, sq)
                    act(ss, ss, AF.Sqrt, scale=1.0 / 128, bias=B_NEPS)
                    rs = SM.v(164, 1)
                    recip(rs, ss)
                    stt(mb(YMK_O + tb * 512 + h * 128, 128), hm, rs, BR[l].v(h * 128, 128), ALU.mult, ALU.mult)
                    psC = psum()
                    mm(psC.v(0, 129), ktok, va)
                    ca = tm().v(0, 129)
                    ts(ca, CS[l].v(h * 129, 129), SM.v(112 + col, 1), None, ALU.mult)
                    stt(CS[l].v(h * 129, 129), psC.v(0, 129), SM.v(112 + col, 1), ca, ALU.mult, ALU.add)
                    act(CSB[l].v(h * 129, 129), CS[l].v(h * 129, 129), AF.Copy)
                for c in range(4):
                    pT = psumT()
                    tr(pT.v(0, 128), mb(YMK_O + tb * 512 + c * 128, 128), identB)
                    act(YM.v(c * 512 + tb * 128, 128), pT.v(0, 128), AF.Copy)

            chk(2)
            whq = wload("win", l, 2056, 512)
            whf = wload("win", l, 2568, 512)
            for h in range(4):
                sq_ = tm().v(0, T)
                proj_fm(whq, 512, h, 8, xbk, lambda psv, sq_=sq_: act(sq_, psv, AF.Silu))
                f_ = tm().v(0, T)
                proj_fm(whf, 512, h, 8, xbk, lambda psv, f_=f_: act(f_, psv, AF.Sigmoid))
                ts(f_, f_, LC[l].v(4 + h, 1), LC[l].v(h, 1), ALU.mult, ALU.add)
                lf_ = tm().v(0, T)
                act(lf_, f_, AF.Ln)
                Bl = tm().v(0, T)
                scan(Bl, RM.v(0, T), lf_, 0.0, ALU.mult, ALU.add)
                Eq = tm().v(0, T)
                Ek = lf_
                act(Eq, Bl, AF.Exp)
                act(Ek, Bl, AF.Exp, scale=-1.0)
                copy(SM.v(176 + h * 8, 8), V(Eq.t, Eq.name, 0, 512, fn=lambda a: a[:, 63::64]))
                tt(mb(QH_O + h * 512, 512), sq_, Eq, ALU.mult)
                ts(f_, f_, -1.0, 1.0, ALU.mult, ALU.add)
                tt(mb(KH_O + h * 512, 512), f_, Ek, ALU.mult)
            w = wload("win", l, 3080, 512)
            for c in range(8):
                ps = psum()
                for kc in range(8):
                    mm(ps.v(0, 512, p1=64), XB.v(kc * T + c * 64, 64), w.v(kc * 512, 512), start=(kc == 0), stop=(kc == 7))
                act(mb(VH_O + c * 512, 512, p1=64), ps.v(0, 512, p1=64), AF.Copy)
            w = wload("win", l, 3592, 512)
            for j in range(4):
                proj_fm(w, 512, j, 8, xbk, lambda psv, j=j: act(mb(SGH_O + j * 512, 512), psv, AF.Sigmoid))
            for c in range(8):
                par = c % 2
                qbs = [mb(QH_O + h * 512 + c * 64, 64) for h in range(4)]
                kbs = [mb(KH_O + h * 512 + c * 64, 64) for h in range(4)]
                vvs = [mb(VH_O + c * 512 + h * 128, 128, p1=64) for h in range(4)]
                psA = psum()
                for h in range(4):
                    mm(psA.v(h * 64, 64, p1=64), kbs[h], qbs[h])
                tt(AS4.v(par * 256, 256, p1=64), psA.v(0, 256, p1=64), MK4.v(0, 256, p1=64), ALU.mult)
                pT = psumT()
                for h in range(4):
                    tr(pT.v(h * 128, 128, p1=64), kbs[h], identB)
                act(KT4.v(par * 512, 512, p1=64), pT.v(0, 512, p1=64), AF.Copy)
                psO = psum()
                for h in range(4):
                    mm(psO.v(h * 128, 128, p1=64), AS4.v(par * 256 + h * 64, 64, p1=64), vvs[h], start=True, stop=False)
                    mm(psO.v(h * 128, 128, p1=64), qbs[h], HSB[l].v(h * 128, 128), start=False, stop=True)
                psS = psum()
                for h in range(4):
                    mm(psS.v(h * 128, 128), KT4.v(par * 512 + h * 128, 128, p1=64), vvs[h])
                tt(HS[l].v(0, 512), HS[l].v(0, 512), psS.v(0, 512), ALU.add)
                for h in range(4):
                    ts(HS[l].v(h * 128, 128), HS[l].v(h * 128, 128), SM.v(176 + h * 8 + c, 1), None, ALU.mult)
                act(HSB[l].v(0, 512), HS[l].v(0, 512), AF.Copy)
                sq = tm().v(0, 512, p1=64)
                act(sq, psO.v(0, 512, p1=64), AF.Square)
                ss4 = SM.v(208 + par * 4, 4, p1=64)
                rs4 = SM.v(216 + par * 4, 4, p1=64)
                rsum(ss4, V(sq.t, sq.name, 0, 512, p1=64, fn=lambda a: a.rearrange("p (h e) -> p h e", h=4)))
                act(ss4, ss4, AF.Sqrt, scale=1.0 / 128, bias=CB.v(3, 1, p1=64))
                recip(rs4, ss4)
                for h in range(4):
                    stt(YHK.v(c * 512 + h * 128, 128, p1=64), psO.v(h * 128, 128, p1=64), SM.v(216 + par * 4 + h, 1, p1=64),
                        BR[l].v(512 + h * 128, 128, p1=64), ALU.mult, ALU.mult)
                pT2 = psumT()
                for j in range(4):
                    tr(pT2.v(j * 64, 64), YHK.v(c * 512 + j * 128, 128, p1=64), IDB.v(0, 64, p1=64))
                j4 = lambda a, c=c: a.rearrange("p (j t) -> p j t", j=4)[:, :, c * 64:(c + 1) * 64]
                tt(YH.v(0, 2048, fn=j4), pT2.v(0, 256, fn=lambda a: a.rearrange("p (j t) -> p j t", j=4)),
                   mb(SGH_O, 2048, fn=j4), ALU.mult)
            chk(3)
            wrx = wload("win", l, 4104, 512)
            wrg = wload("win", l, 4616, 512)
            for c in range(4):
                pre = tm()
                proj_fm(wrx, 512, c, 8, xbk, lambda psv, pre=pre: act(pre.v(4, T), psv, AF.Copy))
                copy(pre.v(1, 3), HALR[l].v(c * 3, 3))
                u = tm().v(0, T)
                ts(u, pre.v(4, T), cv.v(40 + c * 4 + 3, 1), cv.v(56 + c, 1), ALU.mult, ALU.add)
                for jj in (2, 1, 0):
                    stt(u, pre.v(1 + jj, T), cv.v(40 + c * 4 + jj, 1), u, ALU.mult, ALU.add)
                copy(HALR[l].v(c * 3, 3), pre.v(513, 3))
                ub = SMB.v(0, 512)
                act(ub, u, AF.Copy)
                psr = psum()
                mm(psr.v(0, T), RW[l].v(c * 128, 128), ub)
                psi = psum()
                mm(psi.v(0, T), RW[l].v(512 + c * 128, 128), ub)
                r_ = tm().v(0, T)
                i_ = tm().v(0, T)
                act(r_, psr.v(0, T), AF.Sigmoid, bias=cv.v(60 + c, 1))
                act(i_, psi.v(0, T), AF.Sigmoid, bias=cv.v(64 + c, 1))
                a_ = tm().v(0, T)
                th = tm().v(0, T)
                act(a_, r_, AF.Exp, scale=LC[l].v(8 + c, 1))
                act(th, r_, AF.Tanh, scale=LC[l].v(12 + c, 1))
                ts(r_, th, 1.0, None, ALU.add)
                recip(r_, r_)
                tt(th, th, r_, ALU.mult)
                act(th, th, AF.Sqrt, scale=2.0)
                tt(i_, i_, u, ALU.mult)
                tt(i_, i_, th, ALU.mult)
                hh = tm().v(0, T)
                scan(hh, a_, i_, RST[l].v(c, 1), ALU.mult, ALU.add)
                copy(RST[l].v(c, 1), V(hh.t, hh.name, T - 1, T))
                ge = u
                proj_fm(wrg, 512, c, 8, xbk, lambda psv, ge=ge: act(ge, psv, AF.Gelu))
                tt(YR.v(c * 512, 512), hh, ge, ALU.mult)

            chk(4)
            for bi, (bn, Y) in enumerate((("wbm", YM), ("wbh", YH), ("wbr", YR))):
                wb_ = wload(bn, l, 0, 1024)
                for half in range(2):
                    wgt = wload("win", l, 5128 + bi * 1024 + half * 512, 512)
                    for jj in range(4):
                        j = half * 4 + jj
                        sg = tm().v(0, T)
                        proj_fm(wgt, 512, jj, 8, xbk, lambda psv, sg=sg: act(sg, psv, AF.Sigmoid))
                        psP = psum()
                        for kc in range(4):
                            mm(psP.v(0, T), wb_.v(kc * 1024 + j * 128, 128), Y.v(kc * 512, 512), start=(kc == 0), stop=(kc == 3))
                        if bi == 0:
                            tt(MIX.v(j * T, T), sg, psP.v(0, T), ALU.mult)
                        else:
                            tt(sg, sg, psP.v(0, T), ALU.mult)
                            tt(MIX.v(j * T, T), MIX.v(j * T, T), sg, ALU.add)
                        if bi == 2:
                            act(MIXB.v(j * T, T), MIX.v(j * T, T), AF.Copy)
            for half in range(2):
                wo = wload("wout", l, half * 512, 512)
                for jj in range(4):
                    j = half * 4 + jj
                    proj_fm(wo, 512, jj, 8, lambda kc: MIXB.v(kc * T, T),
                            lambda psv, j=j: stt(XT.v(j * T, T), XT.v(j * T, T), ALPHA, psv, ALU.mult, ALU.add))
            layer_norm(l, 72, 80)

            chk(5)
            for g in range(6):
                n = 512 if g < 5 else 256
                wg_ = wload("wg", l, g * 512, n)
                wu_ = wload("wu", l, g * 512, n)
                for jj in range(n // 128):
                    j = g * 4 + jj
                    sg = tm().v(0, T)
                    proj_fm(wg_, n, jj, 8, xbk, lambda psv, sg=sg: act(sg, psv, AF.Silu))
                    proj_fm(wu_, n, jj, 8, xbk, lambda psv, sg=sg, j=j: tt(mb(j * T, T), sg, psv, ALU.mult))
            for j in range(8):
                wd_ = wload("wd", l, j * 128, 128)
                proj_fm(wd_, 128, 0, 22, lambda kc: mb(kc * T, T),
                        lambda psv, j=j: stt(XT.v(j * T, T), XT.v(j * T, T), ALPHA, psv, ALU.mult, ALU.add))
            layer_norm(l, 88, 96)
          except _Stop:
            pass

        osrc = V(XT.t, XT.name, 0, 8 * T, fn=lambda a: a.rearrange("p (k n) -> p k n", k=8))
        S.dma("pool", (lambda e, osrc=osrc, t0=t0: e.dma_start(
            out=outT_d.rearrange("(k p) s -> p k s", p=128)[:, :, t0:t0 + T], in_=osrc.ap)),
            reads=[osrc], writes=[], semkey=f"o{ti % 2}")

    S.wait_all("pool", [(k, v) for k, v in S.dmacnt.items() if k.startswith("o")])

    EP = 16000
    sems = {}
    for k in sorted(S.semkeys):
        if k in Sched.ENG:
            for ep in range(S.cnt[k] // EP + 1):
                sems[(k, ep)] = es.enter_context(nc.semaphore(f"sem_{k}_{ep}"))
        else:
            sems[k] = es.enter_context(nc.semaphore(f"sem_{k}"))
    engmap = {"pe": "tensor", "act": "scalar", "dve": "vector", "pool": "gpsimd", "sp": "sync"}
    with nc.Block() as block:
        for ename, bname in engmap.items():
            def body(e, ename=ename):
                for waits, fn, inc in S.ops[ename]:
                    for k, v in waits:
                        if k in Sched.ENG:
                            e.wait_ge(sems[(k, (v - 1) // EP)], (v - 1) % EP + 1)
                        else:
                            e.wait_ge(sems[k], v)
                    if fn is not None:
                        ins = fn(e)
                        if inc is not None:
                            if inc[0] in Sched.ENG:
                                ins.then_inc(sems[(inc[0], (inc[1] - 1) // EP)], 1)
                            else:
                                ins.then_inc(sems[inc[0]], inc[1])
            getattr(block, bname)(body)
    es.close()
    stats = {k: len(v) for k, v in S.ops.items()}
    return nc, stats


def _consts():
    c = np.zeros((128, 512), np.float32)
    c[:, 0:128] = np.triu(np.ones((128, 128), np.float32))
    c[:, 128:256] = np.eye(128, dtype=np.float32)
    c[:, 256:384] = 1.0 / 1024.0
    c[:, 384:512] = 1.0
    return c


def _pack(inp, n_layers):
    f = lambda a: np.asarray(a, np.float32)
    cvs, brs, rws = [], [], []
    hlb = f(inp["h_lower_bounds"])
    for l in range(n_layers):
        cv = np.zeros((128, NCV), np.float32)
        cv[:, 0:32] = f(inp["m_conv_w"])[l].reshape(4, 8, 128).transpose(2, 1, 0).reshape(128, 32)
        cv[:, 32:40] = f(inp["m_conv_b"])[l].reshape(8, 128).T
        cv[:, 40:56] = f(inp["r_conv_w"])[l].reshape(4, 4, 128).transpose(2, 1, 0).reshape(128, 16)
        cv[:, 56:60] = f(inp["r_conv_b"])[l].reshape(4, 128).T
        cv[:, 60:64] = f(inp["r_b_rec"])[l].reshape(4, 128).T
        cv[:, 64:68] = f(inp["r_b_in"])[l].reshape(4, 128).T
        cv[:, 68:72] = f(inp["r_lambda"])[l].reshape(4, 128).T
        cv[:, 72:80] = f(inp["ln1_g"])[l].reshape(8, 128).T
        cv[:, 80:88] = f(inp["ln1_b"])[l].reshape(8, 128).T
        cv[:, 88:96] = f(inp["ln2_g"])[l].reshape(8, 128).T
        cv[:, 96:104] = f(inp["ln2_b"])[l].reshape(8, 128).T
        cv[:, 104:108] = hlb[0].reshape(4, 128).T
        cv[:, 108:112] = hlb[min(1, hlb.shape[0] - 1)].reshape(4, 128).T
        cvs.append(cv)
        br = np.zeros((128, 1056), np.float32)
        br[:, 0:512] = f(inp["m_norm_g"])[l][None, :]
        br[:, 512:1024] = f(inp["h_norm_g"])[l][None, :]
        bif = np.concatenate([f(inp["m_bias_i"])[l], f(inp["m_bias_f"])[l]])
        br[:, 1024:1056] = np.tile(bif, 4)[None, :]
        brs.append(br)
        rw = np.concatenate([f(inp["r_w_rec"])[l].transpose(1, 0, 2).reshape(128, 512),
                             f(inp["r_w_in"])[l].transpose(1, 0, 2).reshape(128, 512)], axis=1)
        rws.append(rw)
    return np.stack(cvs), np.stack(brs), np.stack(rws)


_CACHE = {}


def run(inputs, S_len, n_layers, seqs, n_cores):
    key = (S_len, n_layers)
    if key not in _CACHE:
        _CACHE[key] = build_program(S_len, n_layers)[0]
    nc = _CACHE[key]
    cv, br, rw = _pack(inputs, n_layers)
    cm = _consts()
    shared = {
        "w_in": np.ascontiguousarray(inputs["w_in"][:n_layers], np.float32),
        "w_branch_m": np.ascontiguousarray(inputs["w_branch_m"][:n_layers], np.float32),
        "w_branch_h": np.ascontiguousarray(inputs["w_branch_h"][:n_layers], np.float32),
        "w_branch_r": np.ascontiguousarray(inputs["w_branch_r"][:n_layers], np.float32),
        "w_out": np.ascontiguousarray(inputs["w_out"][:n_layers], np.float32),
        "w_ff_gate": np.ascontiguousarray(inputs["w_ff_gate"][:n_layers], np.float32),
        "w_ff_up": np.ascontiguousarray(inputs["w_ff_up"][:n_layers], np.float32),
        "w_ff_down": np.ascontiguousarray(inputs["w_ff_down"][:n_layers], np.float32),
        "cvec": cv, "brow": br, "rw": rw, "cmat": cm,
    }
    in_maps = []
    for c in range(n_cores):
        m = dict(shared)
        m["xT"] = np.ascontiguousarray(seqs[c % len(seqs)].T)
        in_maps.append(m)
    res = run_bass_kernel_spmd(nc, in_maps, core_ids=list(range(n_cores)))
    return [np.ascontiguousarray(r["outT"].T) for r in res.results]


def kernel(**inputs):
    x = np.asarray(inputs["x"], np.float32)
    B, S_len, _ = x.shape
    outs = run(inputs, S_len, 2, [x[b] for b in range(B)], 8)
    return np.stack(outs[:B]).astype(np.float32)
```

```python
import math
from contextlib import ExitStack
import numpy as np
import concourse.bass as bass
import concourse.mybir as mybir
from concourse.bass_utils import run_bass_kernel_spmd

F32, BF16 = mybir.dt.float32, mybir.dt.bfloat16
AF = mybir.ActivationFunctionType
ALU = mybir.AluOpType
AX = mybir.AxisListType

D = 1024
DIN = 8200
DFF = 2816
T = 512
ALPHA = 4.0 ** 0.25
LN_EPS = 1e-5
NORM_EPS = 1e-6
NW = 4
WSLOT = 4160
NCV = 112
SAME_ENG_WIN = 8


class V:
    __slots__ = ("t", "name", "lo", "hi", "p0", "p1", "fn")

    def __init__(s, t, name, lo, hi, p0=0, p1=128, fn=None):
        s.t, s.name, s.lo, s.hi, s.p0, s.p1, s.fn = t, name, lo, hi, p0, p1, fn

    @property
    def ap(s):
        a = s.t[s.p0:s.p1, s.lo:s.hi]
        if s.fn is not None:
            a = s.fn(a)
        return a


class Buf:
    def __init__(s, t, name):
        s.t, s.name = t, name

    def v(s, lo, n, p0=0, p1=128, fn=None):
        return V(s.t, s.name, lo, lo + n, p0, p1, fn)


class Sched:
    ENG = ("pe", "act", "dve", "pool", "sp")

    def __init__(s):
        s.ops = {e: [] for e in s.ENG}
        s.cnt = {e: 0 for e in s.ENG}
        s.waited = {e: {} for e in s.ENG}
        s.reg = {}
        s.dmacnt = {}
        s.semkeys = set(s.ENG)

    def _collect(s, need, eng, tok):
        if tok is None:
            return
        k, v = tok
        if k == eng:
            if eng == "pe" or v > s.cnt[eng] or v < s.cnt[eng] - SAME_ENG_WIN:
                return
        if s.waited[eng].get(k, 0) >= v:
            return
        if need.get(k, 0) < v:
            need[k] = v

    def _deps(s, eng, reads, writes, tok):
        need = {}
        for r in reads:
            for ent in s.reg.get(r.name, ()):
                if ent[0] < r.hi and r.lo < ent[1]:
                    s._collect(need, eng, ent[2])
        for w in writes:
            for ent in s.reg.get(w.name, ()):
                if ent[0] < w.hi and w.lo < ent[1]:
                    s._collect(need, eng, ent[2])
                    for k, v in ent[3].items():
                        s._collect(need, eng, (k, v))
        for k, v in need.items():
            s.waited[eng][k] = v
        return sorted(need.items())

    def _record(s, reads, writes, tok):
        for r in reads:
            lst = s.reg.setdefault(r.name, [])
            hit = False
            for ent in lst:
                if ent[0] < r.hi and r.lo < ent[1]:
                    if ent[3].get(tok[0], 0) < tok[1]:
                        ent[3][tok[0]] = tok[1]
                    if ent[0] <= r.lo and r.hi <= ent[1]:
                        hit = True
            if not hit:
                lst.append([r.lo, r.hi, None, {tok[0]: tok[1]}])
        for w in writes:
            lst = s.reg.setdefault(w.name, [])
            lst[:] = [ent for ent in lst if not (w.lo <= ent[0] and ent[1] <= w.hi)]
            lst.append([w.lo, w.hi, tok, {}])

    @staticmethod
    def _bank(reads, writes):
        r2, w2 = [], []
        for r in reads:
            if r.name[0] == "P":
                w2.append(V(None, r.name, 0, 1 << 30))
            else:
                r2.append(r)
        for w in writes:
            w2.append(V(None, w.name, 0, 1 << 30) if w.name[0] == "P" else w)
        return r2, w2

    def op(s, eng, fn, reads=(), writes=(), sig=True):
        reads, writes = s._bank(reads, writes)
        waits = s._deps(eng, reads, writes, None)
        if sig:
            s.cnt[eng] += 1
            tok = (eng, s.cnt[eng])
            inc = (eng, s.cnt[eng])
        else:
            tok = (eng, s.cnt[eng] + 1)
            inc = None
        s._record(reads, writes, tok)
        s.ops[eng].append((waits, fn, inc))

    def dma(s, eng, fn, reads, writes, semkey):
        waits = s._deps(eng, reads, writes, None)
        s.semkeys.add(semkey)
        s.dmacnt[semkey] = s.dmacnt.get(semkey, 0) + 16
        tok = (semkey, s.dmacnt[semkey])
        s._record(reads, writes, tok)
        s.ops[eng].append((waits, fn, (semkey, 16)))

    def wait_all(s, eng, toks):
        s.ops[eng].append((sorted(toks), None, None))


class _Stop(Exception):
    pass


def build_program(S_len, n_layers=2, stage=99):
    assert S_len % T == 0
    ntiles = S_len // T
    nc = bass.Bass("TRN2", target_bir_lowering=False)
    es = ExitStack()
    S = Sched()

    def dram(name, shape, dt, kind):
        return nc.dram_tensor(name, list(shape), dt, kind=kind).ap()

    xT_d = dram("xT", [D, S_len], F32, "ExternalInput")
    outT_d = dram("outT", [D, S_len], F32, "ExternalOutput")
    w_in_d = dram("w_in", [n_layers, D, DIN], F32, "ExternalInput")
    wbm_d = dram("w_branch_m", [n_layers, 512, D], F32, "ExternalInput")
    wbh_d = dram("w_branch_h", [n_layers, 512, D], F32, "ExternalInput")
    wbr_d = dram("w_branch_r", [n_layers, 512, D], F32, "ExternalInput")
    wout_d = dram("w_out", [n_layers, D, D], F32, "ExternalInput")
    wg_d = dram("w_ff_gate", [n_layers, D, DFF], F32, "ExternalInput")
    wu_d = dram("w_ff_up", [n_layers, D, DFF], F32, "ExternalInput")
    wd_d = dram("w_ff_down", [n_layers, DFF, D], F32, "ExternalInput")
    cvec_d = dram("cvec", [n_layers, 128, NCV], F32, "ExternalInput")
    brow_d = dram("brow", [n_layers, 128, 1056], F32, "ExternalInput")
    rw_d = dram("rw", [n_layers, 128, 1024], F32, "ExternalInput")
    cmat_d = dram("cmat", [128, 512], F32, "ExternalInput")

    wspec = [("win", w_in_d, 8, DIN), ("wbm", wbm_d, 4, D), ("wbh", wbh_d, 4, D), ("wbr", wbr_d, 4, D),
             ("wout", wout_d, 8, D), ("wg", wg_d, 8, DFF), ("wu", wu_d, 8, DFF), ("wd", wd_d, 22, D)]
    scr = {}
    for l in range(n_layers):
        for (nm, src, k, n) in wspec:
            t = nc.dram_tensor(f"s_{nm}{l}", [128, k * n], BF16, kind="Internal").ap()
            scr[(nm, l)] = (t, f"s_{nm}{l}", k, n, src)

    def sb(name, n, dt=F32):
        t = es.enter_context(nc.sbuf_tensor(name, [128, n], dt))
        return Buf(t, name)

    XT = sb("XT", 8 * T)
    XB = sb("XB", 8 * T, BF16)
    WB = [sb(f"WB{i}", WSLOT, BF16) for i in range(NW)]
    NTM = 12
    TM = [sb(f"TM{i}", 516) for i in range(NTM)]
    MIX = sb("MIX", 8 * T)
    MB = sb("MBF", 22 * T, BF16)
    YM = sb("YMT", 4 * T, BF16)
    YH = sb("YHT", 4 * T, BF16)
    YR = sb("YRT", 4 * T, BF16)
    MIXB = sb("MIXB", 8 * T, BF16)
    CST = sb("CST", 512)
    IDB = sb("IDB", 128, BF16)
    ONES = sb("ONES", 512)
    CV = [sb(f"CV{l}", NCV) for l in range(n_layers)]
    BR = [sb(f"BR{l}", 1056) for l in range(n_layers)]
    RW = [sb(f"RW{l}", 1024, BF16) for l in range(n_layers)]
    LC = [sb(f"LC{l}", 32) for l in range(n_layers)]
    CS = [sb(f"CS{l}", 4 * 129) for l in range(n_layers)]
    CSB = [sb(f"CSB{l}", 4 * 129, BF16) for l in range(n_layers)]
    HS = [sb(f"HS{l}", 4 * 128) for l in range(n_layers)]
    HALM = [sb(f"HALM{l}", 8 * 3) for l in range(n_layers)]
    HALR = [sb(f"HALR{l}", 4 * 3) for l in range(n_layers)]
    RST = [sb(f"RST{l}", 4) for l in range(n_layers)]
    SM = sb("SM", 512)
    SMB = sb("SMB", 1024, BF16)
    AS4 = sb("AS4", 512, BF16)
    KT4 = sb("KT4", 1024, BF16)
    HSB = [sb(f"HSB{l}", 512, BF16) for l in range(n_layers)]
    RM = sb("RM", 512)
    MK4 = sb("MK4", 256)

    PS = [Buf(es.enter_context(nc.psum_tensor(f"PS{i}", [128, 512], F32)), f"PS{i}") for i in range(6)]
    PT = [Buf(es.enter_context(nc.psum_tensor(f"PT{i}", [128, 1024], BF16)), f"PT{i}") for i in range(2)]
    rr = {"ps": 0, "pt": 0, "w": 0, "tm": 0}

    def psum():
        rr["ps"] = (rr["ps"] + 1) % 6
        return PS[rr["ps"]]

    def psumT():
        rr["pt"] = (rr["pt"] + 1) % 2
        return PT[rr["pt"]]

    def mb(lo, n, **kw):
        return MB.v(lo, n, **kw)
    QT_O, KT_O, VA_O, OG_O, YMK_O = 0, 2048, 4096, 4096 + 2064, 4096 + 2064 + 2048
    QH_O, KH_O, VH_O, SGH_O, YHK_O = 0, 2048, 4096, 8192, 10240
    YHK = sb("YHK", 8 * T, BF16)

    maskU = CST.v(0, 128)
    identF = CST.v(128, 128)
    onesM = CST.v(256, 128)
    ones1 = CST.v(384, 128)

    def mm(out, lhsT, rhs, start=True, stop=True):
        S.op("pe", lambda e: e.matmul(out.ap, lhsT=lhsT.ap, rhs=rhs.ap, start=start, stop=stop),
             reads=[lhsT, rhs], writes=[out], sig=stop)

    def tr(out, in_, ident):
        S.op("pe", lambda e: e.transpose(out.ap, in_.ap, ident.ap), reads=[in_, ident], writes=[out])

    def act(out, in_, func, bias=None, scale=None):
        rd = [in_]
        kw = {}
        if bias is not None:
            if isinstance(bias, V):
                rd.append(bias)
            kw["bias"] = bias
        if scale is not None:
            if isinstance(scale, V):
                rd.append(scale)
            kw["scale"] = scale

        def f(e):
            k2 = {k: (v.ap if isinstance(v, V) else v) for k, v in kw.items()}
            return e.activation(out=out.ap, in_=in_.ap, func=func, **k2)
        S.op("act", f, reads=rd, writes=[out])

    def ts(out, in0, s1, s2, op0, op1=None, eng="dve"):
        rd = [in0] + [x for x in (s1, s2) if isinstance(x, V)]

        def f(e):
            a1 = s1.ap if isinstance(s1, V) else s1
            a2 = s2.ap if isinstance(s2, V) else s2
            if op1 is None:
                return e.tensor_scalar(out=out.ap, in0=in0.ap, scalar1=a1, scalar2=None, op0=op0)
            return e.tensor_scalar(out=out.ap, in0=in0.ap, scalar1=a1, scalar2=a2, op0=op0, op1=op1)
        S.op(eng, f, reads=rd, writes=[out])

    def stt(out, in0, scalar, in1, op0, op1):
        rd = [in0, in1] + ([scalar] if isinstance(scalar, V) else [])

        def f(e):
            sc = scalar.ap if isinstance(scalar, V) else scalar
            return e.scalar_tensor_tensor(out=out.ap, in0=in0.ap, scalar=sc, in1=in1.ap, op0=op0, op1=op1)
        S.op("dve", f, reads=rd, writes=[out])

    def tt(out, in0, in1, op, eng="dve"):
        S.op(eng, lambda e: e.tensor_tensor(out=out.ap, in0=in0.ap, in1=in1.ap, op=op), reads=[in0, in1], writes=[out])

    def scan(out, d0, d1, init, op0, op1):
        rd = [d0, d1] + ([init] if isinstance(init, V) else [])

        def f(e):
            i = init.ap if isinstance(init, V) else init
            return e.tensor_tensor_scan(out=out.ap, data0=d0.ap, data1=d1.ap, initial=i, op0=op0, op1=op1)
        S.op("dve", f, reads=rd, writes=[out])

    def recip(out, in_):
        S.op("dve", lambda e: e.reciprocal(out=out.ap, in_=in_.ap), reads=[in_], writes=[out])

    def copy(out, in_, eng="dve"):
        S.op(eng, lambda e: e.tensor_copy(out=out.ap, in_=in_.ap), reads=[in_], writes=[out])

    def rsum(out, in_):
        S.op("dve", lambda e: e.reduce_sum(out=out.ap, in_=in_.ap, axis=AX.X), reads=[in_], writes=[out])

    def memset(buf_v, val, eng="dve"):
        S.op(eng, lambda e: e.memset(buf_v.ap, val), reads=[], writes=[buf_v])

    def dma(eng, out, in_, semkey):
        S.dma(eng, lambda e: e.dma_start(out=out.ap, in_=in_.ap), reads=[in_], writes=[out], semkey=semkey)

    def dview(ap, name, n, fn=None):
        return V(ap, name, 0, n, fn=fn)

    for l in range(n_layers):
        if stage < 0:
            break
        for (nm, src, k, n) in wspec:
            t, tname, k, n, src = scr[(nm, l)]
            semkey = f"cast_{nm}{l}"
            for kc in range(k):
                for c0 in range(0, n, 2048):
                    c1 = min(n, c0 + 2048)
                    sap = src[l][kc * 128:(kc + 1) * 128, c0:c1]
                    dst = V(t, tname, kc * n + c0, kc * n + c1)
                    S.dma("pool", (lambda e, dst=dst, sap=sap: e.dma_start(out=dst.ap, in_=sap)), reads=[], writes=[dst],
                          semkey=semkey)
            S.reg[tname] = [[0, k * n, (semkey, S.dmacnt[semkey]), {}]]
    S.dma("sp", lambda e: e.dma_start(out=CST.t[:, :], in_=cmat_d), reads=[], writes=[CST.v(0, 512)], semkey="c0")
    for l in range(n_layers):
        S.dma("sp", (lambda e, l=l: e.dma_start(out=CV[l].t[:, :], in_=cvec_d[l])), reads=[], writes=[CV[l].v(0, NCV)], semkey=f"c1_{l}")
        S.dma("sp", (lambda e, l=l: e.dma_start(out=BR[l].t[:, :], in_=brow_d[l])), reads=[], writes=[BR[l].v(0, 1056)], semkey=f"c2_{l}")
        S.dma("pool", (lambda e, l=l: e.dma_start(out=RW[l].t[:, :], in_=rw_d[l])), reads=[], writes=[RW[l].v(0, 1024)], semkey=f"c3_{l}")
    S.dma("pool", lambda e: e.dma_start(out=IDB.t[:, :], in_=cmat_d[:, 128:256]), reads=[], writes=[IDB.v(0, 128)], semkey="c4")
    identB = IDB.v(0, 128)
    CB = sb("CB", 8)
    memset(CB.v(0, 1), 1.0)
    memset(CB.v(1, 1), math.log(128.0 ** -0.5))
    memset(CB.v(2, 1), LN_EPS)
    memset(CB.v(3, 1), NORM_EPS)
    B_ONE, B_LNS, B_LNEPS, B_NEPS = CB.v(0, 1), CB.v(1, 1), CB.v(2, 1), CB.v(3, 1)
    memset(ONES.v(0, 512), 1.0)
    memset(RM.v(0, 512), 1.0)
    memset(RM.v(0, 449, fn=lambda a: a[:, ::64]), 0.0)
    for h in range(4):
        copy(MK4.v(h * 64, 64, p1=64), CST.v(0, 64, p1=64))
    for l in range(n_layers):
        memset(CS[l].v(0, 516), 0.0)
        memset(CSB[l].v(0, 516), 0.0)
        memset(HS[l].v(0, 512), 0.0)
        memset(HSB[l].v(0, 512), 0.0)
        memset(HALM[l].v(0, 24), 0.0)
        memset(HALR[l].v(0, 12), 0.0)
        memset(RST[l].v(0, 4), 0.0)
    e0 = SM.v(0, 4)
    e1 = SM.v(4, 4)
    esum = SM.v(8, 4)
    act(e0, CV[0].v(104, 4), AF.Exp)
    act(e1, CV[0].v(108, 4), AF.Exp)
    tt(esum, e0, e1, ALU.add)
    recip(esum, esum)
    memset(LC[0].v(0, 4), 0.0)
    if n_layers > 1:
        tt(LC[1].v(0, 4), e1, esum, ALU.mult)
    for l in range(n_layers):
        ts(LC[l].v(4, 4), LC[l].v(0, 4), -1.0, 1.0, ALU.mult, ALU.add)
        act(SM.v(12, 4), CV[l].v(68, 4), AF.Exp, scale=-1.0)
        act(SM.v(16, 4), SM.v(12, 4), AF.Ln, bias=B_ONE)
        ts(LC[l].v(8, 4), SM.v(16, 4), -8.0, None, ALU.mult)
        ts(LC[l].v(12, 4), SM.v(16, 4), 8.0, None, ALU.mult)

    def wload(nm, l, c0, n):
        t, tname, k, ntot, _ = scr[(nm, l)]
        rr["w"] = (rr["w"] + 1) % NW
        slot = WB[rr["w"]]
        src = V(t, tname, 0, k * ntot, fn=lambda a: a.rearrange("p (k n) -> p k n", k=k)[:, :, c0:c0 + n])
        dst = V(slot.t, slot.name, 0, k * n, fn=lambda a: a.rearrange("p (k n) -> p k n", k=k))
        dma("sp", dst, src, f"w{rr['w']}")
        return slot

    def tm():
        rr["tm"] = (rr["tm"] + 1) % NTM
        return TM[rr["tm"]]

    LNS = math.log(128.0 ** -0.5)

    def chk(k):
        if stage < k:
            raise _Stop()

    def proj_fm(slot, n, j, nk, rhs_of_k, evac):
        ps = psum()
        for kc in range(nk):
            mm(ps.v(0, T), slot.v(kc * n + j * 128, 128), rhs_of_k(kc), start=(kc == 0), stop=(kc == nk - 1))
        evac(ps.v(0, T))

    xbk = lambda kc: XB.v(kc * T, T)

    def layer_norm(l, goff, boff):
        s1 = psum()
        s2 = psum()
        for j in range(8):
            zq = tm()
            act(zq.v(0, T), XT.v(j * T, T), AF.Square)
            mm(s1.v(0, T), onesM, XT.v(j * T, T), start=(j == 0), stop=(j == 7))
            mm(s2.v(0, T), onesM, zq.v(0, T), start=(j == 0), stop=(j == 7))
        mean = tm().v(0, T)
        var = tm().v(0, T)
        rstd = tm().v(0, T)
        nmr = tm().v(0, T)
        act(mean, s1.v(0, T), AF.Copy)
        tt(var, mean, mean, ALU.mult)
        tt(var, s2.v(0, T), var, ALU.subtract)
        act(var, var, AF.Sqrt, bias=B_LNEPS)
        recip(rstd, var)
        stt(nmr, mean, -1.0, rstd, ALU.mult, ALU.mult)
        for j in range(8):
            t1 = tm().v(0, T)
            g = CV[l].v(goff + j, 1)
            b = CV[l].v(boff + j, 1)
            stt(t1, XT.v(j * T, T), g, rstd, ALU.mult, ALU.mult)
            stt(t1, nmr, g, t1, ALU.mult, ALU.add)
            ts(XT.v(j * T, T), t1, b, None, ALU.add)
            act(XB.v(j * T, T), t1, AF.Identity, bias=b)

    for ti in range(ntiles):
        t0 = ti * T
        xsrc = V(xT_d, "xT_d", 0, 1, fn=lambda a, t0=t0: a.rearrange("(k p) s -> p k s", p=128)[:, :, t0:t0 + T])
        xsrc.fn = xsrc.fn
        xdst = V(XT.t, XT.name, 0, 8 * T, fn=lambda a: a.rearrange("p (k n) -> p k n", k=8))
        S.dma("pool", (lambda e, xdst=xdst, t0=t0: e.dma_start(
            out=xdst.ap, in_=xT_d.rearrange("(k p) s -> p k s", p=128)[:, :, t0:t0 + T])),
            reads=[], writes=[xdst], semkey=f"x{ti % 2}")
        for j in range(8):
            act(XB.v(j * T, T), XT.v(j * T, T), AF.Copy)

        for l in range(n_layers):
          try:
            cv = CV[l]
            chk(1)
            w = wload("win", l, 1536, 520)
            OG = lambda tb, h: mb(OG_O + tb * 512 + h * 128, 128)
            for tb in range(4):
                ps = psum()
                for kc in range(8):
                    mm(ps.v(0, 512), XB.v(kc * T + tb * 128, 128), w.v(kc * 520, 512), start=(kc == 0), stop=(kc == 7))
                act(mb(OG_O + tb * 512, 512), ps.v(0, 512), AF.Sigmoid)
            psg = psum()
            for tb in range(4):
                for kc in range(8):
                    mm(psg.v(tb * 8, 8), XB.v(kc * T + tb * 128, 128), w.v(kc * 520 + 512, 8), start=(kc == 0), stop=(kc == 7))
            chk(1.1)
            gz = SM.v(32, 32)
            tt(gz, psg.v(0, 32), BR[l].v(1024, 32), ALU.add)
            v3 = lambda a, lo: a.rearrange("p (a b) -> p a b", b=8)[:, :, lo:lo + 4]
            li = SM.v(32, 32, fn=lambda a: v3(a, 0))
            zf = SM.v(32, 32, fn=lambda a: v3(a, 4))
            c34 = lambda a: a.rearrange("p (a b) -> p a b", b=4)
            sp_ = SM.v(64, 16)
            chk(1.11)
            act(SM.v(64, 16, fn=c34), zf, AF.Exp, scale=-1.0)
            chk(1.12)
            sp0 = sp_
            sp_ = SM.v(144, 16)
            act(sp_, sp0, AF.Ln, bias=B_ONE)
            chk(1.2)
            psb = psum()
            mm(psb.v(0, 16), maskU, sp_)
            mm(psb.v(16, 16), ones1, sp_)
            aP = SM.v(80, 16)
            beta = SM.v(96, 16)
            aL = SM.v(112, 16)
            act(aP, psb.v(0, 16), AF.Exp, scale=-1.0, bias=B_LNS)
            tt(SM.v(128, 16, fn=c34), li, psb.v(0, 16, fn=c34), ALU.add)
            act(beta, SM.v(128, 16), AF.Exp)
            act(aL, psb.v(16, 16), AF.Exp, scale=-1.0)
            chk(1.3)
            w = wload("win", l, 1024, 512)
            for tb in range(4):
                ps = psum()
                for kc in range(8):
                    mm(ps.v(0, 512), XB.v(kc * T + tb * 128, 128), w.v(kc * 512, 512), start=(kc == 0), stop=(kc == 7))
                for h in range(4):
                    act(mb(VA_O + (tb * 4 + h) * 129, 128), ps.v(h * 128, 128), AF.Identity, scale=SM.v(96 + tb * 4 + h, 1))
            copy(mb(VA_O, 16 * 129, fn=lambda a: a.rearrange("p (a b) -> p a b", b=129)[:, :, 128:129]),
                 SM.v(96, 16, fn=lambda a: a.rearrange("p (a b) -> p a b", b=1)))
            chk(1.4)
            for g in range(2):
                w = wload("win", l, g * 512, 512)
                for j in range(4):
                    c = g * 4 + j
                    pre = tm()

                    def ev(psv, pre=pre, c=c):
                        act(pre.v(4, T), psv, AF.Copy)
                    proj_fm(w, 512, j, 8, xbk, ev)
                    copy(pre.v(1, 3), HALM[l].v(c * 3, 3))
                    acc = tm().v(0, T)
                    ts(acc, pre.v(4, T), cv.v(c * 4 + 3, 1), cv.v(32 + c, 1), ALU.mult, ALU.add)
                    for jj in (2, 1, 0):
                        stt(acc, pre.v(1 + jj, T), cv.v(c * 4 + jj, 1), acc, ALU.mult, ALU.add)
                    copy(HALM[l].v(c * 3, 3), pre.v(513, 3))
                    act(mb((QT_O if g == 0 else KT_O) + j * 512, 512), acc, AF.Silu)
            chk(1.5)

            def m_part1(tb, h):
                qb = mb(QT_O + h * 512 + tb * 128, 128)
                kb = mb(KT_O + h * 512 + tb * 128, 128)
                va = mb(VA_O + (tb * 4 + h) * 129, 129)
                col = tb * 4 + h
                psA = psum()
                mm(psA.v(0, 128), kb, qb)
                sts = SMB.v(((tb * 4 + h) % 2) * 128, 128)
                tt(sts, psA.v(0, 128), maskU, ALU.mult)
                pT = psumT()
                tr(pT.v(0, 128), kb, identB)
                ktok = SMB.v(256 + ((tb * 4 + h) % 2) * 128, 128)
                act(ktok, pT.v(0, 128), AF.Copy)
                psB = psum()
                mm(psB.v(0, 129), sts, va, start=True, stop=False)
                mm(psB.v(0, 129), qb, CSB[l].v(h * 129, 129), start=False, stop=True)
                psC = psum()
                mm(psC.v(0, 129), ktok, va)
                ca = tm().v(0, 129)
                ts(ca, CS[l].v(h * 129, 129), SM.v(112 + col, 1), None, ALU.mult)
                stt(CS[l].v(h * 129, 129), psC.v(0, 129), SM.v(112 + col, 1), ca, ALU.mult, ALU.add)
                act(CSB[l].v(h * 129, 129), CS[l].v(h * 129, 129), AF.Copy)
                return psB

            def m_part2(tb, h, psB):
                col = tb * 4 + h
                par = col % 2
                d1 = SM.v(160 + par * 8, 1)
                d2 = SM.v(161 + par * 8, 1)
                rr_ = SM.v(162 + par * 8, 1)
                act(d1, psB.v(128, 1), AF.Abs, scale=SM.v(80 + col, 1))
                ts(d1, d1, 1.0, None, ALU.max)
                recip(d2, d1)
                tt(rr_, d2, SM.v(80 + col, 1), ALU.mult)
                hm = tm().v(0, 128)
                stt(hm, psB.v(0, 128), rr_, OG(tb, h), ALU.mult, ALU.mult)
                sq = tm().v(0, 128)
                act(sq, hm, AF.Square)
                ss = SM.v(163 + par * 8, 1)
                rsum(ss, sq)
                act(ss, ss, AF.Sqrt, scale=1.0 / 128, bias=B_NEPS)
                rs = SM.v(164 + par * 8, 1)
                recip(rs, ss)
                stt(mb(YMK_O + tb * 512 + h * 128, 128), hm, rs, BR[l].v(h * 128, 128), ALU.mult, ALU.mult)

            def m_part2b(tb):
                for c in range(4):
                    pT = psumT()
                    tr(pT.v(0, 128), mb(YMK_O + tb * 512 + c * 128, 128), identB)
                    act(YM.v(c * 512 + tb * 128, 128), pT.v(0, 128), AF.Copy)

            steps = [(tb, h) for tb in range(4) for h in range(4)]
            prev = None
            defer = []
            for (tb, h) in steps:
                pb = m_part1(tb, h)
                if defer and defer[0][0] <= 0:
                    m_part2b(defer.pop(0)[1])
                defer = [(d - 1, t_) for (d, t_) in defer]
                if prev is not None:
                    m_part2(*prev)
                    if prev[1] == 3:
                        defer.append((1, prev[0]))
                prev = (tb, h, pb)
            m_part2(*prev)
            for (_, t_) in defer:
                m_part2b(t_)
            m_part2b(3)
            chk(2)
            whq = wload("win", l, 2056, 512)
            whf = wload("win", l, 2568, 512)
            for hp in range(2):
                hs_ = (2 * hp, 2 * hp + 1)
                sqd, ffd, lfd, bld, eqd = {}, {}, {}, {}, {}
                for h in hs_:
                    sqd[h] = tm().v(0, T)
                    ffd[h] = tm().v(0, T)

                    def ev_q(psv, h=h):
                        act(sqd[h], psv, AF.Sigmoid)
                        tt(sqd[h], sqd[h], psv, ALU.mult)
                    proj_fm(whq, 512, h, 8, xbk, ev_q)
                    proj_fm(whf, 512, h, 8, xbk, lambda psv, h=h: act(ffd[h], psv, AF.Sigmoid))
                    ts(ffd[h], ffd[h], LC[l].v(4 + h, 1), LC[l].v(h, 1), ALU.mult, ALU.add)
                for h in hs_:
                    lfd[h] = tm().v(0, T)
                    act(lfd[h], ffd[h], AF.Ln)
                for h in hs_:
                    bld[h] = tm().v(0, T)
                    scan(bld[h], RM.v(0, T), lfd[h], 0.0, ALU.mult, ALU.add)
                for h in hs_:
                    eqd[h] = tm().v(0, T)
                    act(eqd[h], bld[h], AF.Exp)
                    act(lfd[h], bld[h], AF.Exp, scale=-1.0)
                for h in hs_:
                    Eq = eqd[h]
                    copy(SM.v(176 + h * 8, 8), V(Eq.t, Eq.name, 0, 512, fn=lambda a: a[:, 63::64]))
                    tt(mb(QH_O + h * 512, 512), sqd[h], Eq, ALU.mult)
                    ts(ffd[h], ffd[h], -1.0, 1.0, ALU.mult, ALU.add)
                    tt(mb(KH_O + h * 512, 512), ffd[h], lfd[h], ALU.mult)
            w = wload("win", l, 3080, 512)
            for c in range(8):
                ps = psum()
                for kc in range(8):
                    mm(ps.v(0, 512, p1=64), XB.v(kc * T + c * 64, 64), w.v(kc * 512, 512), start=(kc == 0), stop=(kc == 7))
                act(mb(VH_O + c * 512, 512, p1=64), ps.v(0, 512, p1=64), AF.Copy)
            w = wload("win", l, 3592, 512)
            for j in range(4):
                proj_fm(w, 512, j, 8, xbk, lambda psv, j=j: act(mb(SGH_O + j * 512, 512), psv, AF.Sigmoid))
            def h_part1(c):
                par = c % 2
                qbs = [mb(QH_O + h * 512 + c * 64, 64) for h in range(4)]
                kbs = [mb(KH_O + h * 512 + c * 64, 64) for h in range(4)]
                vvs = [mb(VH_O + c * 512 + h * 128, 128, p1=64) for h in range(4)]
                psA = psum()
                for h in range(4):
                    mm(psA.v(h * 64, 64, p1=64), kbs[h], qbs[h])
                tt(AS4.v(par * 256, 256, p1=64), psA.v(0, 256, p1=64), MK4.v(0, 256, p1=64), ALU.mult)
                pT = psumT()
                for h in range(4):
                    tr(pT.v(h * 128, 128, p1=64), kbs[h], identB)
                act(KT4.v(par * 512, 512, p1=64), pT.v(0, 512, p1=64), AF.Copy)
                psO = psum()
                for h in range(4):
                    mm(psO.v(h * 128, 128, p1=64), AS4.v(par * 256 + h * 64, 64, p1=64), vvs[h], start=True, stop=False)
                    mm(psO.v(h * 128, 128, p1=64), qbs[h], HSB[l].v(h * 128, 128), start=False, stop=True)
                psS = psum()
                for h in range(4):
                    mm(psS.v(h * 128, 128), KT4.v(par * 512 + h * 128, 128, p1=64), vvs[h])
                tt(HS[l].v(0, 512), HS[l].v(0, 512), psS.v(0, 512), ALU.add)
                for h in range(4):
                    ts(HS[l].v(h * 128, 128), HS[l].v(h * 128, 128), SM.v(176 + h * 8 + c, 1), None, ALU.mult)
                act(HSB[l].v(0, 512), HS[l].v(0, 512), AF.Copy)
                return psO

            def h_part2(c, psO):
                par = c % 2
                sq = tm().v(0, 512, p1=64)
                act(sq, psO.v(0, 512, p1=64), AF.Square)
                ss4 = SM.v(208 + par * 4, 4, p1=64)
                rs4 = SM.v(216 + par * 4, 4, p1=64)
                rsum(ss4, V(sq.t, sq.name, 0, 512, p1=64, fn=lambda a: a.rearrange("p (h e) -> p h e", h=4)))
                act(ss4, ss4, AF.Sqrt, scale=1.0 / 128, bias=CB.v(3, 1, p1=64))
                recip(rs4, ss4)
                for h in range(4):
                    stt(YHK.v(c * 512 + h * 128, 128, p1=64), psO.v(h * 128, 128, p1=64), SM.v(216 + par * 4 + h, 1, p1=64),
                        BR[l].v(512 + h * 128, 128, p1=64), ALU.mult, ALU.mult)

            def h_part2b(c):
                pT2 = psumT()
                for j in range(4):
                    tr(pT2.v(j * 64, 64), YHK.v(c * 512 + j * 128, 128, p1=64), IDB.v(0, 64, p1=64))
                j4 = lambda a, c=c: a.rearrange("p (j t) -> p j t", j=4)[:, :, c * 64:(c + 1) * 64]
                tt(YH.v(0, 2048, fn=j4), pT2.v(0, 256, fn=lambda a: a.rearrange("p (j t) -> p j t", j=4)),
                   mb(SGH_O, 2048, fn=j4), ALU.mult)
            pos_ = {}
            for c in range(8):
                pos_[c] = h_part1(c)
                if c >= 1:
                    h_part2(c - 1, pos_[c - 1])
                if c >= 2:
                    h_part2b(c - 2)
            h_part2(7, pos_[7])
            h_part2b(6)
            h_part2b(7)
            chk(3)
            wrx = wload("win", l, 4104, 512)
            wrg = wload("win", l, 4616, 512)
            for c in range(4):
                pre = tm()
                proj_fm(wrx, 512, c, 8, xbk, lambda psv, pre=pre: act(pre.v(4, T), psv, AF.Copy))
                copy(pre.v(1, 3), HALR[l].v(c * 3, 3))
                u = tm().v(0, T)
                ts(u, pre.v(4, T), cv.v(40 + c * 4 + 3, 1), cv.v(56 + c, 1), ALU.mult, ALU.add)
                for jj in (2, 1, 0):
                    stt(u, pre.v(1 + jj, T), cv.v(40 + c * 4 + jj, 1), u, ALU.mult, ALU.add)
                copy(HALR[l].v(c * 3, 3), pre.v(513, 3))
                ub = SMB.v(0, 512)
                act(ub, u, AF.Copy)
                psr = psum()
                mm(psr.v(0, T), RW[l].v(c * 128, 128), ub)
                psi = psum()
                mm(psi.v(0, T), RW[l].v(512 + c * 128, 128), ub)
                ge = tm().v(0, T)

                def ev_g(psv, ge=ge):
                    act(ge, psv, AF.Erf, scale=2.0 ** -0.5)
                    ts(ge, ge, 1.0, 0.5, ALU.add, ALU.mult)
                    tt(ge, ge, psv, ALU.mult)
                proj_fm(wrg, 512, c, 8, xbk, ev_g)
                r_ = tm().v(0, T)
                i_ = tm().v(0, T)
                act(r_, psr.v(0, T), AF.Sigmoid, bias=cv.v(60 + c, 1))
                act(i_, psi.v(0, T), AF.Sigmoid, bias=cv.v(64 + c, 1))
                a_ = tm().v(0, T)
                th = tm().v(0, T)
                act(th, r_, AF.Tanh, scale=LC[l].v(12 + c, 1))
                ts(r_, th, 1.0, None, ALU.add)
                recip(r_, r_)
                ts(a_, th, -1.0, 1.0, ALU.mult, ALU.add)
                tt(a_, a_, r_, ALU.mult)
                tt(th, th, r_, ALU.mult)
                act(a_, a_, AF.Sqrt)
                act(th, th, AF.Sqrt, scale=2.0)
                tt(i_, i_, u, ALU.mult)
                tt(i_, i_, th, ALU.mult)
                hh = tm().v(0, T)
                scan(hh, a_, i_, RST[l].v(c, 1), ALU.mult, ALU.add)
                copy(RST[l].v(c, 1), V(hh.t, hh.name, T - 1, T))
                tt(YR.v(c * 512, 512), hh, ge, ALU.mult)

            chk(4)
            for bi, (bn, Y) in enumerate((("wbm", YM), ("wbh", YH), ("wbr", YR))):
                wb_ = wload(bn, l, 0, 1024)
                for half in range(2):
                    wgt = wload("win", l, 5128 + bi * 1024 + half * 512, 512)
                    for jj in range(4):
                        j = half * 4 + jj
                        sg = tm().v(0, T)
                        proj_fm(wgt, 512, jj, 8, xbk, lambda psv, sg=sg: act(sg, psv, AF.Sigmoid))
                        psP = psum()
                        for kc in range(4):
                            mm(psP.v(0, T), wb_.v(kc * 1024 + j * 128, 128), Y.v(kc * 512, 512), start=(kc == 0), stop=(kc == 3))
                        if bi == 0:
                            tt(MIX.v(j * T, T), sg, psP.v(0, T), ALU.mult)
                        else:
                            tt(sg, sg, psP.v(0, T), ALU.mult)
                            tt(MIX.v(j * T, T), MIX.v(j * T, T), sg, ALU.add)
                        if bi == 2:
                            act(MIXB.v(j * T, T), MIX.v(j * T, T), AF.Copy)
            for half in range(2):
                wo = wload("wout", l, half * 512, 512)
                for jj in range(4):
                    j = half * 4 + jj
                    proj_fm(wo, 512, jj, 8, lambda kc: MIXB.v(kc * T, T),
                            lambda psv, j=j: stt(XT.v(j * T, T), XT.v(j * T, T), ALPHA, psv, ALU.mult, ALU.add))
            layer_norm(l, 72, 80)

            chk(5)
            for g in range(6):
                n = 512 if g < 5 else 256
                wg_ = wload("wg", l, g * 512, n)
                wu_ = wload("wu", l, g * 512, n)
                for jj in range(n // 128):
                    j = g * 4 + jj
                    sg = tm().v(0, T)
                    proj_fm(wg_, n, jj, 8, xbk, lambda psv, sg=sg: act(sg, psv, AF.Silu))
                    proj_fm(wu_, n, jj, 8, xbk, lambda psv, sg=sg, j=j: tt(mb(j * T, T), sg, psv, ALU.mult))
            for j in range(8):
                wd_ = wload("wd", l, j * 128, 128)
                proj_fm(wd_, 128, 0, 22, lambda kc: mb(kc * T, T),
                        lambda psv, j=j: stt(XT.v(j * T, T), XT.v(j * T, T), ALPHA, psv, ALU.mult, ALU.add))
            layer_norm(l, 88, 96)
          except _Stop:
            pass

        osrc = V(XT.t, XT.name, 0, 8 * T, fn=lambda a: a.rearrange("p (k n) -> p k n", k=8))
        S.dma("pool", (lambda e, osrc=osrc, t0=t0: e.dma_start(
            out=outT_d.rearrange("(k p) s -> p k s", p=128)[:, :, t0:t0 + T], in_=osrc.ap)),
            reads=[osrc], writes=[], semkey=f"o{ti % 2}")

    S.wait_all("pool", [(k, v) for k, v in S.dmacnt.items() if k.startswith("o")])

    EP = 16000
    sems = {}
    for k in sorted(S.semkeys):
        if k in Sched.ENG:
            for ep in range(S.cnt[k] // EP + 1):
                sems[(k, ep)] = es.enter_context(nc.semaphore(f"sem_{k}_{ep}"))
        else:
            sems[k] = es.enter_context(nc.semaphore(f"sem_{k}"))
    engmap = {"pe": "tensor", "act": "scalar", "dve": "vector", "pool": "gpsimd", "sp": "sync"}
    with nc.Block() as block:
        for ename, bname in engmap.items():
            def body(e, ename=ename):
                for waits, fn, inc in S.ops[ename]:
                    for k, v in waits:
                        if k in Sched.ENG:
                            e.wait_ge(sems[(k, (v - 1) // EP)], (v - 1) % EP + 1)
                        else:
                            e.wait_ge(sems[k], v)
                    if fn is not None:
                        ins = fn(e)
                        if inc is not None:
                            if inc[0] in Sched.ENG:
                                ins.then_inc(sems[(inc[0], (inc[1] - 1) // EP)], 1)
                            else:
                                ins.then_inc(sems[inc[0]], inc[1])
            getattr(block, bname)(body)
    es.close()
    stats = {k: len(v) for k, v in S.ops.items()}
    return nc, stats


def _consts():
    c = np.zeros((128, 512), np.float32)
    c[:, 0:128] = np.triu(np.ones((128, 128), np.float32))
    c[:, 128:256] = np.eye(128, dtype=np.float32)
    c[:, 256:384] = 1.0 / 1024.0
    c[:, 384:512] = 1.0
    return c


def _pack(inp, n_layers):
    f = lambda a: np.asarray(a, np.float32)
    cvs, brs, rws = [], [], []
    hlb = f(inp["h_lower_bounds"])
    for l in range(n_layers):
        cv = np.zeros((128, NCV), np.float32)
        cv[:, 0:32] = f(inp["m_conv_w"])[l].reshape(4, 8, 128).transpose(2, 1, 0).reshape(128, 32)
        cv[:, 32:40] = f(inp["m_conv_b"])[l].reshape(8, 128).T
        cv[:, 40:56] = f(inp["r_conv_w"])[l].reshape(4, 4, 128).transpose(2, 1, 0).reshape(128, 16)
        cv[:, 56:60] = f(inp["r_conv_b"])[l].reshape(4, 128).T
        cv[:, 60:64] = f(inp["r_b_rec"])[l].reshape(4, 128).T
        cv[:, 64:68] = f(inp["r_b_in"])[l].reshape(4, 128).T
        cv[:, 68:72] = f(inp["r_lambda"])[l].reshape(4, 128).T
        cv[:, 72:80] = f(inp["ln1_g"])[l].reshape(8, 128).T
        cv[:, 80:88] = f(inp["ln1_b"])[l].reshape(8, 128).T
        cv[:, 88:96] = f(inp["ln2_g"])[l].reshape(8, 128).T
        cv[:, 96:104] = f(inp["ln2_b"])[l].reshape(8, 128).T
        cv[:, 104:108] = hlb[0].reshape(4, 128).T
        cv[:, 108:112] = hlb[min(1, hlb.shape[0] - 1)].reshape(4, 128).T
        cvs.append(cv)
        br = np.zeros((128, 1056), np.float32)
        br[:, 0:512] = f(inp["m_norm_g"])[l][None, :]
        br[:, 512:1024] = f(inp["h_norm_g"])[l][None, :]
        bif = np.concatenate([f(inp["m_bias_i"])[l], f(inp["m_bias_f"])[l]])
        br[:, 1024:1056] = np.tile(bif, 4)[None, :]
        brs.append(br)
        rw = np.concatenate([f(inp["r_w_rec"])[l].transpose(1, 0, 2).reshape(128, 512),
                             f(inp["r_w_in"])[l].transpose(1, 0, 2).reshape(128, 512)], axis=1)
        rws.append(rw)
    return np.stack(cvs), np.stack(brs), np.stack(rws)


_CACHE = {}


def run(inputs, S_len, n_layers, seqs, n_cores):
    key = (S_len, n_layers)
    if key not in _CACHE:
        _CACHE[key] = build_program(S_len, n_layers)[0]
    nc = _CACHE[key]
    cv, br, rw = _pack(inputs, n_layers)
    cm = _consts()
    shared = {
        "w_in": np.ascontiguousarray(inputs["w_in"][:n_layers], np.float32),
        "w_branch_m": np.ascontiguousarray(inputs["w_branch_m"][:n_layers], np.float32),
        "w_branch_h": np.ascontiguousarray(inputs["w_branch_h"][:n_layers], np.float32),
        "w_branch_r": np.ascontiguousarray(inputs["w_branch_r"][:n_layers], np.float32),
        "w_out": np.ascontiguousarray(inputs["w_out"][:n_layers], np.float32),
        "w_ff_gate": np.ascontiguousarray(inputs["w_ff_gate"][:n_layers], np.float32),
        "w_ff_up": np.ascontiguousarray(inputs["w_ff_up"][:n_layers], np.float32),
        "w_ff_down": np.ascontiguousarray(inputs["w_ff_down"][:n_layers], np.float32),
        "cvec": cv, "brow": br, "rw": rw, "cmat": cm,
    }
    in_maps = []
    for c in range(n_cores):
        m = dict(shared)
        m["xT"] = np.ascontiguousarray(seqs[c % len(seqs)].T)
        in_maps.append(m)
    res = run_bass_kernel_spmd(nc, in_maps, core_ids=list(range(n_cores)))
    return [np.ascontiguousarray(r["outT"].T) for r in res.results]


def kernel(**inputs):
    x = np.asarray(inputs["x"], np.float32)
    B, S_len, _ = x.shape
    outs = run(inputs, S_len, 2, [x[b] for b in range(B)], 8)
    return np.stack(outs[:B]).astype(np.float32)
```

```python
import math
from contextlib import ExitStack
import numpy as np
import concourse.bass as bass
import concourse.mybir as mybir
from concourse.bass_utils import run_bass_kernel_spmd

F32, BF16 = mybir.dt.float32, mybir.dt.bfloat16
AF = mybir.ActivationFunctionType
ALU = mybir.AluOpType
AX = mybir.AxisListType

D = 1024
DIN = 8200
DFF = 2816
T = 512
ALPHA = 4.0 ** 0.25
LN_EPS = 1e-5
NORM_EPS = 1e-6
NW = 4
WSLOT = 4160
NCV = 112
SAME_ENG_WIN = 8


class V:
    __slots__ = ("t", "name", "lo", "hi", "p0", "p1", "fn")

    def __init__(s, t, name, lo, hi, p0=0, p1=128, fn=None):
        s.t, s.name, s.lo, s.hi, s.p0, s.p1, s.fn = t, name, lo, hi, p0, p1, fn

    @property
    def ap(s):
        a = s.t[s.p0:s.p1, s.lo:s.hi]
        if s.fn is not None:
            a = s.fn(a)
        return a


class Buf:
    def __init__(s, t, name):
        s.t, s.name = t, name

    def v(s, lo, n, p0=0, p1=128, fn=None):
        return V(s.t, s.name, lo, lo + n, p0, p1, fn)


class Sched:
    ENG = ("pe", "act", "dve", "pool", "sp")

    def __init__(s):
        s.ops = {e: [] for e in s.ENG}
        s.cnt = {e: 0 for e in s.ENG}
        s.waited = {e: {} for e in s.ENG}
        s.reg = {}
        s.dmacnt = {}
        s.semkeys = set(s.ENG)

    def _collect(s, need, eng, tok):
        if tok is None:
            return
        k, v = tok
        if k == eng:
            if eng == "pe" or v > s.cnt[eng] or v < s.cnt[eng] - SAME_ENG_WIN:
                return
        if s.waited[eng].get(k, 0) >= v:
            return
        if need.get(k, 0) < v:
            need[k] = v

    def _deps(s, eng, reads, writes, tok):
        need = {}
        for r in reads:
            for ent in s.reg.get(r.name, ()):
                if ent[0] < r.hi and r.lo < ent[1]:
                    s._collect(need, eng, ent[2])
        for w in writes:
            for ent in s.reg.get(w.name, ()):
                if ent[0] < w.hi and w.lo < ent[1]:
                    s._collect(need, eng, ent[2])
                    for k, v in ent[3].items():
                        s._collect(need, eng, (k, v))
        for k, v in need.items():
            s.waited[eng][k] = v
        return sorted(need.items())

    def _record(s, reads, writes, tok):
        for r in reads:
            lst = s.reg.setdefault(r.name, [])
            hit = False
            for ent in lst:
                if ent[0] < r.hi and r.lo < ent[1]:
                    if ent[3].get(tok[0], 0) < tok[1]:
                        ent[3][tok[0]] = tok[1]
                    if ent[0] <= r.lo and r.hi <= ent[1]:
                        hit = True
            if not hit:
                lst.append([r.lo, r.hi, None, {tok[0]: tok[1]}])
        for w in writes:
            lst = s.reg.setdefault(w.name, [])
            lst[:] = [ent for ent in lst if not (w.lo <= ent[0] and ent[1] <= w.hi)]
            lst.append([w.lo, w.hi, tok, {}])

    @staticmethod
    def _bank(reads, writes):
        r2, w2 = [], []
        for r in reads:
            if r.name[0] == "P":
                w2.append(V(None, r.name, 0, 1 << 30))
            else:
                r2.append(r)
        for w in writes:
            w2.append(V(None, w.name, 0, 1 << 30) if w.name[0] == "P" else w)
        return r2, w2

    def op(s, eng, fn, reads=(), writes=(), sig=True):
        reads, writes = s._bank(reads, writes)
        waits = s._deps(eng, reads, writes, None)
        if sig:
            s.cnt[eng] += 1
            tok = (eng, s.cnt[eng])
            inc = (eng, s.cnt[eng])
        else:
            tok = (eng, s.cnt[eng] + 1)
            inc = None
        s._record(reads, writes, tok)
        s.ops[eng].append((waits, fn, inc))

    def dma(s, eng, fn, reads, writes, semkey):
        waits = s._deps(eng, reads, writes, None)
        s.semkeys.add(semkey)
        s.dmacnt[semkey] = s.dmacnt.get(semkey, 0) + 16
        tok = (semkey, s.dmacnt[semkey])
        s._record(reads, writes, tok)
        s.ops[eng].append((waits, fn, (semkey, 16)))

    def wait_all(s, eng, toks):
        s.ops[eng].append((sorted(toks), None, None))


class _Stop(Exception):
    pass


def build_program(S_len, n_layers=2, stage=99):
    assert S_len % T == 0
    ntiles = S_len // T
    nc = bass.Bass("TRN2", target_bir_lowering=False)
    es = ExitStack()
    S = Sched()

    def dram(name, shape, dt, kind):
        return nc.dram_tensor(name, list(shape), dt, kind=kind).ap()

    xT_d = dram("xT", [D, S_len], F32, "ExternalInput")
    outT_d = dram("outT", [D, S_len], F32, "ExternalOutput")
    w_in_d = dram("w_in", [n_layers, D, DIN], F32, "ExternalInput")
    wbm_d = dram("w_branch_m", [n_layers, 512, D], F32, "ExternalInput")
    wbh_d = dram("w_branch_h", [n_layers, 512, D], F32, "ExternalInput")
    wbr_d = dram("w_branch_r", [n_layers, 512, D], F32, "ExternalInput")
    wout_d = dram("w_out", [n_layers, D, D], F32, "ExternalInput")
    wg_d = dram("w_ff_gate", [n_layers, D, DFF], F32, "ExternalInput")
    wu_d = dram("w_ff_up", [n_layers, D, DFF], F32, "ExternalInput")
    wd_d = dram("w_ff_down", [n_layers, DFF, D], F32, "ExternalInput")
    cvec_d = dram("cvec", [n_layers, 128, NCV], F32, "ExternalInput")
    brow_d = dram("brow", [n_layers, 128, 1056], F32, "ExternalInput")
    rw_d = dram("rw", [n_layers, 128, 1024], F32, "ExternalInput")
    cmat_d = dram("cmat", [128, 512], F32, "ExternalInput")

    wspec = [("win", w_in_d, 8, DIN), ("wbm", wbm_d, 4, D), ("wbh", wbh_d, 4, D), ("wbr", wbr_d, 4, D),
             ("wout", wout_d, 8, D), ("wg", wg_d, 8, DFF), ("wu", wu_d, 8, DFF), ("wd", wd_d, 22, D)]
    scr = {}
    for l in range(n_layers):
        for (nm, src, k, n) in wspec:
            t = nc.dram_tensor(f"s_{nm}{l}", [128, k * n], BF16, kind="Internal").ap()
            scr[(nm, l)] = (t, f"s_{nm}{l}", k, n, src)

    def sb(name, n, dt=F32):
        t = es.enter_context(nc.sbuf_tensor(name, [128, n], dt))
        return Buf(t, name)

    XT = sb("XT", 8 * T)
    XB = sb("XB", 8 * T, BF16)
    WB = [sb(f"WB{i}", WSLOT, BF16) for i in range(NW)]
    NTM = 12
    TM = [sb(f"TM{i}", 516) for i in range(NTM)]
    MIX = sb("MIX", 8 * T)
    MB = sb("MBF", 22 * T, BF16)
    YM = sb("YMT", 4 * T, BF16)
    YH = sb("YHT", 4 * T, BF16)
    YR = sb("YRT", 4 * T, BF16)
    MIXB = sb("MIXB", 8 * T, BF16)
    CST = sb("CST", 512)
    IDB = sb("IDB", 128, BF16)
    ONES = sb("ONES", 512)
    CV = [sb(f"CV{l}", NCV) for l in range(n_layers)]
    BR = [sb(f"BR{l}", 1056) for l in range(n_layers)]
    RW = [sb(f"RW{l}", 1024, BF16) for l in range(n_layers)]
    LC = [sb(f"LC{l}", 32) for l in range(n_layers)]
    CS = [sb(f"CS{l}", 4 * 129) for l in range(n_layers)]
    CSB = [sb(f"CSB{l}", 4 * 129, BF16) for l in range(n_layers)]
    HS = [sb(f"HS{l}", 4 * 128) for l in range(n_layers)]
    HALM = [sb(f"HALM{l}", 8 * 3) for l in range(n_layers)]
    HALR = [sb(f"HALR{l}", 4 * 3) for l in range(n_layers)]
    RST = [sb(f"RST{l}", 4) for l in range(n_layers)]
    SM = sb("SM", 512)
    SMB = sb("SMB", 1024, BF16)
    AS4 = sb("AS4", 512, BF16)
    KT4 = sb("KT4", 1024, BF16)
    HSB = [sb(f"HSB{l}", 512, BF16) for l in range(n_layers)]
    RM = sb("RM", 512)
    MK4 = sb("MK4", 256)

    PS = [Buf(es.enter_context(nc.psum_tensor(f"PS{i}", [128, 512], F32)), f"PS{i}") for i in range(6)]
    PT = [Buf(es.enter_context(nc.psum_tensor(f"PT{i}", [128, 1024], BF16)), f"PT{i}") for i in range(2)]
    rr = {"ps": 0, "pt": 0, "w": 0, "tm": 0}

    def psum():
        rr["ps"] = (rr["ps"] + 1) % 6
        return PS[rr["ps"]]

    def psumT():
        rr["pt"] = (rr["pt"] + 1) % 2
        return PT[rr["pt"]]

    def mb(lo, n, **kw):
        return MB.v(lo, n, **kw)
    QT_O, KT_O, VA_O, OG_O, YMK_O = 0, 2048, 4096, 4096 + 2064, 4096 + 2064 + 2048
    QH_O, KH_O, VH_O, SGH_O, YHK_O = 0, 2048, 4096, 8192, 10240
    YHK = sb("YHK", 8 * T, BF16)

    maskU = CST.v(0, 128)
    identF = CST.v(128, 128)
    onesM = CST.v(256, 128)
    ones1 = CST.v(384, 128)

    def mm(out, lhsT, rhs, start=True, stop=True):
        S.op("pe", lambda e: e.matmul(out.ap, lhsT=lhsT.ap, rhs=rhs.ap, start=start, stop=stop),
             reads=[lhsT, rhs], writes=[out], sig=stop)

    def tr(out, in_, ident):
        S.op("pe", lambda e: e.transpose(out.ap, in_.ap, ident.ap), reads=[in_, ident], writes=[out])

    def act(out, in_, func, bias=None, scale=None):
        rd = [in_]
        kw = {}
        if bias is not None:
            if isinstance(bias, V):
                rd.append(bias)
            kw["bias"] = bias
        if scale is not None:
            if isinstance(scale, V):
                rd.append(scale)
            kw["scale"] = scale

        def f(e):
            k2 = {k: (v.ap if isinstance(v, V) else v) for k, v in kw.items()}
            return e.activation(out=out.ap, in_=in_.ap, func=func, **k2)
        S.op("act", f, reads=rd, writes=[out])

    def ts(out, in0, s1, s2, op0, op1=None, eng="dve"):
        rd = [in0] + [x for x in (s1, s2) if isinstance(x, V)]

        def f(e):
            a1 = s1.ap if isinstance(s1, V) else s1
            a2 = s2.ap if isinstance(s2, V) else s2
            if op1 is None:
                return e.tensor_scalar(out=out.ap, in0=in0.ap, scalar1=a1, scalar2=None, op0=op0)
            return e.tensor_scalar(out=out.ap, in0=in0.ap, scalar1=a1, scalar2=a2, op0=op0, op1=op1)
        S.op(eng, f, reads=rd, writes=[out])

    def stt(out, in0, scalar, in1, op0, op1):
        rd = [in0, in1] + ([scalar] if isinstance(scalar, V) else [])

        def f(e):
            sc = scalar.ap if isinstance(scalar, V) else scalar
            return e.scalar_tensor_tensor(out=out.ap, in0=in0.ap, scalar=sc, in1=in1.ap, op0=op0, op1=op1)
        S.op("dve", f, reads=rd, writes=[out])

    def tt(out, in0, in1, op, eng="dve"):
        S.op(eng, lambda e: e.tensor_tensor(out=out.ap, in0=in0.ap, in1=in1.ap, op=op), reads=[in0, in1], writes=[out])

    def scan(out, d0, d1, init, op0, op1):
        rd = [d0, d1] + ([init] if isinstance(init, V) else [])

        def f(e):
            i = init.ap if isinstance(init, V) else init
            return e.tensor_tensor_scan(out=out.ap, data0=d0.ap, data1=d1.ap, initial=i, op0=op0, op1=op1)
        S.op("dve", f, reads=rd, writes=[out])

    def recip(out, in_):
        S.op("dve", lambda e: e.reciprocal(out=out.ap, in_=in_.ap), reads=[in_], writes=[out])

    def copy(out, in_, eng="dve"):
        S.op(eng, lambda e: e.tensor_copy(out=out.ap, in_=in_.ap), reads=[in_], writes=[out])

    def rsum(out, in_):
        S.op("dve", lambda e: e.reduce_sum(out=out.ap, in_=in_.ap, axis=AX.X), reads=[in_], writes=[out])

    def memset(buf_v, val, eng="dve"):
        S.op(eng, lambda e: e.memset(buf_v.ap, val), reads=[], writes=[buf_v])

    def dma(eng, out, in_, semkey):
        S.dma(eng, lambda e: e.dma_start(out=out.ap, in_=in_.ap), reads=[in_], writes=[out], semkey=semkey)

    def dview(ap, name, n, fn=None):
        return V(ap, name, 0, n, fn=fn)

    for l in range(n_layers):
        if stage < 0:
            break
        for (nm, src, k, n) in wspec:
            t, tname, k, n, src = scr[(nm, l)]
            semkey = f"cast_{nm}{l}"
            for kc in range(k):
                for c0 in range(0, n, 2048):
                    c1 = min(n, c0 + 2048)
                    sap = src[l][kc * 128:(kc + 1) * 128, c0:c1]
                    dst = V(t, tname, kc * n + c0, kc * n + c1)
                    S.dma("pool", (lambda e, dst=dst, sap=sap: e.dma_start(out=dst.ap, in_=sap)), reads=[], writes=[dst],
                          semkey=semkey)
            S.reg[tname] = [[0, k * n, (semkey, S.dmacnt[semkey]), {}]]
    S.dma("sp", lambda e: e.dma_start(out=CST.t[:, :], in_=cmat_d), reads=[], writes=[CST.v(0, 512)], semkey="c0")
    for l in range(n_layers):
        S.dma("sp", (lambda e, l=l: e.dma_start(out=CV[l].t[:, :], in_=cvec_d[l])), reads=[], writes=[CV[l].v(0, NCV)], semkey=f"c1_{l}")
        S.dma("sp", (lambda e, l=l: e.dma_start(out=BR[l].t[:, :], in_=brow_d[l])), reads=[], writes=[BR[l].v(0, 1056)], semkey=f"c2_{l}")
        S.dma("pool", (lambda e, l=l: e.dma_start(out=RW[l].t[:, :], in_=rw_d[l])), reads=[], writes=[RW[l].v(0, 1024)], semkey=f"c3_{l}")
    S.dma("pool", lambda e: e.dma_start(out=IDB.t[:, :], in_=cmat_d[:, 128:256]), reads=[], writes=[IDB.v(0, 128)], semkey="c4")
    identB = IDB.v(0, 128)
    CB = sb("CB", 8)
    memset(CB.v(0, 1), 1.0)
    memset(CB.v(1, 1), math.log(128.0 ** -0.5))
    memset(CB.v(2, 1), LN_EPS)
    memset(CB.v(3, 1), NORM_EPS)
    B_ONE, B_LNS, B_LNEPS, B_NEPS = CB.v(0, 1), CB.v(1, 1), CB.v(2, 1), CB.v(3, 1)
    memset(ONES.v(0, 512), 1.0)
    memset(RM.v(0, 512), 1.0)
    memset(RM.v(0, 449, fn=lambda a: a[:, ::64]), 0.0)
    for h in range(4):
        copy(MK4.v(h * 64, 64, p1=64), CST.v(0, 64, p1=64))
    for l in range(n_layers):
        memset(CS[l].v(0, 516), 0.0)
        memset(CSB[l].v(0, 516), 0.0)
        memset(HS[l].v(0, 512), 0.0)
        memset(HSB[l].v(0, 512), 0.0)
        memset(HALM[l].v(0, 24), 0.0)
        memset(HALR[l].v(0, 12), 0.0)
        memset(RST[l].v(0, 4), 0.0)
    e0 = SM.v(0, 4)
    e1 = SM.v(4, 4)
    esum = SM.v(8, 4)
    act(e0, CV[0].v(104, 4), AF.Exp)
    act(e1, CV[0].v(108, 4), AF.Exp)
    tt(esum, e0, e1, ALU.add)
    recip(esum, esum)
    memset(LC[0].v(0, 4), 0.0)
    if n_layers > 1:
        tt(LC[1].v(0, 4), e1, esum, ALU.mult)
    for l in range(n_layers):
        ts(LC[l].v(4, 4), LC[l].v(0, 4), -1.0, 1.0, ALU.mult, ALU.add)
        act(SM.v(12, 4), CV[l].v(68, 4), AF.Exp, scale=-1.0)
        act(SM.v(16, 4), SM.v(12, 4), AF.Ln, bias=B_ONE)
        ts(LC[l].v(8, 4), SM.v(16, 4), -8.0, None, ALU.mult)
        ts(LC[l].v(12, 4), SM.v(16, 4), 8.0, None, ALU.mult)

    def wload(nm, l, c0, n):
        t, tname, k, ntot, _ = scr[(nm, l)]
        rr["w"] = (rr["w"] + 1) % NW
        slot = WB[rr["w"]]
        src = V(t, tname, 0, k * ntot, fn=lambda a: a.rearrange("p (k n) -> p k n", k=k)[:, :, c0:c0 + n])
        dst = V(slot.t, slot.name, 0, k * n, fn=lambda a: a.rearrange("p (k n) -> p k n", k=k))
        dma("sp", dst, src, f"w{rr['w']}")
        return slot

    def tm():
        rr["tm"] = (rr["tm"] + 1) % NTM
        return TM[rr["tm"]]

    LNS = math.log(128.0 ** -0.5)

    def chk(k):
        if stage < k:
            raise _Stop()

    def proj_fm(slot, n, j, nk, rhs_of_k, evac):
        ps = psum()
        for kc in range(nk):
            mm(ps.v(0, T), slot.v(kc * n + j * 128, 128), rhs_of_k(kc), start=(kc == 0), stop=(kc == nk - 1))
        evac(ps.v(0, T))

    xbk = lambda kc: XB.v(kc * T, T)

    def layer_norm(l, goff, boff):
        s1 = psum()
        s2 = psum()
        for j in range(8):
            zq = tm()
            act(zq.v(0, T), XT.v(j * T, T), AF.Square)
            mm(s1.v(0, T), onesM, XT.v(j * T, T), start=(j == 0), stop=(j == 7))
            mm(s2.v(0, T), onesM, zq.v(0, T), start=(j == 0), stop=(j == 7))
        mean = tm().v(0, T)
        var = tm().v(0, T)
        rstd = tm().v(0, T)
        nmr = tm().v(0, T)
        act(mean, s1.v(0, T), AF.Copy)
        tt(var, mean, mean, ALU.mult)
        tt(var, s2.v(0, T), var, ALU.subtract)
        act(var, var, AF.Sqrt, bias=B_LNEPS)
        recip(rstd, var)
        stt(nmr, mean, -1.0, rstd, ALU.mult, ALU.mult)
        for j in range(8):
            t1 = tm().v(0, T)
            g = CV[l].v(goff + j, 1)
            b = CV[l].v(boff + j, 1)
            stt(t1, XT.v(j * T, T), g, rstd, ALU.mult, ALU.mult)
            stt(t1, nmr, g, t1, ALU.mult, ALU.add)
            ts(XT.v(j * T, T), t1, b, None, ALU.add)
            act(XB.v(j * T, T), t1, AF.Identity, bias=b)

    for ti in range(ntiles):
        t0 = ti * T
        xsrc = V(xT_d, "xT_d", 0, 1, fn=lambda a, t0=t0: a.rearrange("(k p) s -> p k s", p=128)[:, :, t0:t0 + T])
        xsrc.fn = xsrc.fn
        xdst = V(XT.t, XT.name, 0, 8 * T, fn=lambda a: a.rearrange("p (k n) -> p k n", k=8))
        S.dma("pool", (lambda e, xdst=xdst, t0=t0: e.dma_start(
            out=xdst.ap, in_=xT_d.rearrange("(k p) s -> p k s", p=128)[:, :, t0:t0 + T])),
            reads=[], writes=[xdst], semkey=f"x{ti % 2}")
        for j in range(8):
            act(XB.v(j * T, T), XT.v(j * T, T), AF.Copy)

        for l in range(n_layers):
          try:
            cv = CV[l]
            chk(1)
            w = wload("win", l, 1536, 520)
            OG = lambda tb, h: mb(OG_O + tb * 512 + h * 128, 128)
            for tb in range(4):
                ps = psum()
                for kc in range(8):
                    mm(ps.v(0, 512), XB.v(kc * T + tb * 128, 128), w.v(kc * 520, 512), start=(kc == 0), stop=(kc == 7))
                act(mb(OG_O + tb * 512, 512), ps.v(0, 512), AF.Sigmoid)
            psg = psum()
            for tb in range(4):
                for kc in range(8):
                    mm(psg.v(tb * 8, 8), XB.v(kc * T + tb * 128, 128), w.v(kc * 520 + 512, 8), start=(kc == 0), stop=(kc == 7))
            chk(1.1)
            gz = SM.v(32, 32)
            tt(gz, psg.v(0, 32), BR[l].v(1024, 32), ALU.add)
            v3 = lambda a, lo: a.rearrange("p (a b) -> p a b", b=8)[:, :, lo:lo + 4]
            li = SM.v(32, 32, fn=lambda a: v3(a, 0))
            zf = SM.v(32, 32, fn=lambda a: v3(a, 4))
            c34 = lambda a: a.rearrange("p (a b) -> p a b", b=4)
            sp_ = SM.v(64, 16)
            chk(1.11)
            act(SM.v(64, 16, fn=c34), zf, AF.Exp, scale=-1.0)
            chk(1.12)
            sp0 = sp_
            sp_ = SM.v(144, 16)
            act(sp_, sp0, AF.Ln, bias=B_ONE)
            chk(1.2)
            psb = psum()
            mm(psb.v(0, 16), maskU, sp_)
            mm(psb.v(16, 16), ones1, sp_)
            aP = SM.v(80, 16)
            beta = SM.v(96, 16)
            aL = SM.v(112, 16)
            act(aP, psb.v(0, 16), AF.Exp, scale=-1.0, bias=B_LNS)
            tt(SM.v(128, 16, fn=c34), li, psb.v(0, 16, fn=c34), ALU.add)
            act(beta, SM.v(128, 16), AF.Exp)
            act(aL, psb.v(16, 16), AF.Exp, scale=-1.0)
            chk(1.3)
            w = wload("win", l, 1024, 512)
            for tb in range(4):
                ps = psum()
                for kc in range(8):
                    mm(ps.v(0, 512), XB.v(kc * T + tb * 128, 128), w.v(kc * 512, 512), start=(kc == 0), stop=(kc == 7))
                for h in range(4):
                    act(mb(VA_O + (tb * 4 + h) * 129, 128), ps.v(h * 128, 128), AF.Identity, scale=SM.v(96 + tb * 4 + h, 1))
            copy(mb(VA_O, 16 * 129, fn=lambda a: a.rearrange("p (a b) -> p a b", b=129)[:, :, 128:129]),
                 SM.v(96, 16, fn=lambda a: a.rearrange("p (a b) -> p a b", b=1)))
            chk(1.4)
            for g in range(2):
                w = wload("win", l, g * 512, 512)
                for j in range(4):
                    c = g * 4 + j
                    pre = tm()

                    def ev(psv, pre=pre, c=c):
                        act(pre.v(4, T), psv, AF.Copy)
                    proj_fm(w, 512, j, 8, xbk, ev)
                    copy(pre.v(1, 3), HALM[l].v(c * 3, 3))
                    acc = tm().v(0, T)
                    ts(acc, pre.v(4, T), cv.v(c * 4 + 3, 1), cv.v(32 + c, 1), ALU.mult, ALU.add)
                    for jj in (2, 1, 0):
                        stt(acc, pre.v(1 + jj, T), cv.v(c * 4 + jj, 1), acc, ALU.mult, ALU.add)
                    copy(HALM[l].v(c * 3, 3), pre.v(513, 3))
                    act(mb((QT_O if g == 0 else KT_O) + j * 512, 512), acc, AF.Silu)
            chk(1.5)

            def m_part1(tb, h):
                qb = mb(QT_O + h * 512 + tb * 128, 128)
                kb = mb(KT_O + h * 512 + tb * 128, 128)
                va = mb(VA_O + (tb * 4 + h) * 129, 129)
                col = tb * 4 + h
                psA = psum()
                mm(psA.v(0, 128), kb, qb)
                sts = SMB.v(((tb * 4 + h) % 2) * 128, 128)
                tt(sts, psA.v(0, 128), maskU, ALU.mult)
                pT = psumT()
                tr(pT.v(0, 128), kb, identB)
                ktok = SMB.v(256 + ((tb * 4 + h) % 2) * 128, 128)
                act(ktok, pT.v(0, 128), AF.Copy)
                psB = psum()
                mm(psB.v(0, 129), sts, va, start=True, stop=False)
                mm(psB.v(0, 129), qb, CSB[l].v(h * 129, 129), start=False, stop=True)
                psC = psum()
                mm(psC.v(0, 129), ktok, va)
                ca = tm().v(0, 129)
                ts(ca, CS[l].v(h * 129, 129), SM.v(112 + col, 1), None, ALU.mult)
                stt(CS[l].v(h * 129, 129), psC.v(0, 129), SM.v(112 + col, 1), ca, ALU.mult, ALU.add)
                act(CSB[l].v(h * 129, 129), CS[l].v(h * 129, 129), AF.Copy)
                return psB

            def m_part2(tb, h, psB):
                col = tb * 4 + h
                par = col % 2
                d1 = SM.v(160 + par * 8, 1)
                d2 = SM.v(161 + par * 8, 1)
                rr_ = SM.v(162 + par * 8, 1)
                act(d1, psB.v(128, 1), AF.Abs, scale=SM.v(80 + col, 1))
                ts(d1, d1, 1.0, None, ALU.max)
                recip(d2, d1)
                tt(rr_, d2, SM.v(80 + col, 1), ALU.mult)
                hm = tm().v(0, 128)
                stt(hm, psB.v(0, 128), rr_, OG(tb, h), ALU.mult, ALU.mult)
                sq = tm().v(0, 128)
                act(sq, hm, AF.Square)
                ss = SM.v(163 + par * 8, 1)
                rsum(ss, sq)
                act(ss, ss, AF.Sqrt, scale=1.0 / 128, bias=B_NEPS)
                rs = SM.v(164 + par * 8, 1)
                recip(rs, ss)
                stt(mb(YMK_O + tb * 512 + h * 128, 128), hm, rs, BR[l].v(h * 128, 128), ALU.mult, ALU.mult)

            def m_part2b(tb):
                for c in range(4):
                    pT = psumT()
                    tr(pT.v(0, 128), mb(YMK_O + tb * 512 + c * 128, 128), identB)
                    act(YM.v(c * 512 + tb * 128, 128), pT.v(0, 128), AF.Copy)

            steps = [(tb, h) for tb in range(4) for h in range(4)]
            prev = None
            defer = []
            for (tb, h) in steps:
                pb = m_part1(tb, h)
                if defer and defer[0][0] <= 0:
                    m_part2b(defer.pop(0)[1])
                defer = [(d - 1, t_) for (d, t_) in defer]
                if prev is not None:
                    m_part2(*prev)
                    if prev[1] == 3:
                        defer.append((1, prev[0]))
                prev = (tb, h, pb)
            m_part2(*prev)
            for (_, t_) in defer:
                m_part2b(t_)
            m_part2b(3)
            chk(2)
            whq = wload("win", l, 2056, 512)
            whf = wload("win", l, 2568, 512)
            for hp in range(2):
                hs_ = (2 * hp, 2 * hp + 1)
                sqd, ffd, lfd, bld, eqd = {}, {}, {}, {}, {}
                for h in hs_:
                    sqd[h] = tm().v(0, T)
                    ffd[h] = tm().v(0, T)

                    def ev_q(psv, h=h):
                        act(sqd[h], psv, AF.Sigmoid)
                        tt(sqd[h], sqd[h], psv, ALU.mult)
                    proj_fm(whq, 512, h, 8, xbk, ev_q)
                    proj_fm(whf, 512, h, 8, xbk, lambda psv, h=h: act(ffd[h], psv, AF.Sigmoid))
                    ts(ffd[h], ffd[h], LC[l].v(4 + h, 1), LC[l].v(h, 1), ALU.mult, ALU.add)
                for h in hs_:
                    lfd[h] = tm().v(0, T)
                    act(lfd[h], ffd[h], AF.Ln)
                for h in hs_:
                    bld[h] = tm().v(0, T)
                    scan(bld[h], RM.v(0, T), lfd[h], 0.0, ALU.mult, ALU.add)
                for h in hs_:
                    eqd[h] = tm().v(0, T)
                    act(eqd[h], bld[h], AF.Exp)
                    act(lfd[h], bld[h], AF.Exp, scale=-1.0)
                for h in hs_:
                    Eq = eqd[h]
                    copy(SM.v(176 + h * 8, 8), V(Eq.t, Eq.name, 0, 512, fn=lambda a: a[:, 63::64]))
                    tt(mb(QH_O + h * 512, 512), sqd[h], Eq, ALU.mult)
                    ts(ffd[h], ffd[h], -1.0, 1.0, ALU.mult, ALU.add)
                    tt(mb(KH_O + h * 512, 512), ffd[h], lfd[h], ALU.mult)
            w = wload("win", l, 3080, 512)
            for c in range(8):
                ps = psum()
                for kc in range(8):
                    mm(ps.v(0, 512, p1=64), XB.v(kc * T + c * 64, 64), w.v(kc * 512, 512), start=(kc == 0), stop=(kc == 7))
                act(mb(VH_O + c * 512, 512, p1=64), ps.v(0, 512, p1=64), AF.Copy)
            w = wload("win", l, 3592, 512)
            for j in range(4):
                proj_fm(w, 512, j, 8, xbk, lambda psv, j=j: act(mb(SGH_O + j * 512, 512), psv, AF.Sigmoid))
            def h_part1(c):
                par = c % 2
                qbs = [mb(QH_O + h * 512 + c * 64, 64) for h in range(4)]
                kbs = [mb(KH_O + h * 512 + c * 64, 64) for h in range(4)]
                vvs = [mb(VH_O + c * 512 + h * 128, 128, p1=64) for h in range(4)]
                psA = psum()
                for h in range(4):
                    mm(psA.v(h * 64, 64, p1=64), kbs[h], qbs[h])
                tt(AS4.v(par * 256, 256, p1=64), psA.v(0, 256, p1=64), MK4.v(0, 256, p1=64), ALU.mult)
                pT = psumT()
                for h in range(4):
                    tr(pT.v(h * 128, 128, p1=64), kbs[h], identB)
                act(KT4.v(par * 512, 512, p1=64), pT.v(0, 512, p1=64), AF.Copy)
                psO = psum()
                for h in range(4):
                    mm(psO.v(h * 128, 128, p1=64), AS4.v(par * 256 + h * 64, 64, p1=64), vvs[h], start=True, stop=False)
                    mm(psO.v(h * 128, 128, p1=64), qbs[h], HSB[l].v(h * 128, 128), start=False, stop=True)
                psS = psum()
                for h in range(4):
                    mm(psS.v(h * 128, 128), KT4.v(par * 512 + h * 128, 128, p1=64), vvs[h])
                tt(HS[l].v(0, 512), HS[l].v(0, 512), psS.v(0, 512), ALU.add)
                for h in range(4):
                    ts(HS[l].v(h * 128, 128), HS[l].v(h * 128, 128), SM.v(176 + h * 8 + c, 1), None, ALU.mult)
                act(HSB[l].v(0, 512), HS[l].v(0, 512), AF.Copy)
                return psO

            def h_part2(c, psO):
                par = c % 2
                sq = tm().v(0, 512, p1=64)
                act(sq, psO.v(0, 512, p1=64), AF.Square)
                ss4 = SM.v(208 + par * 4, 4, p1=64)
                rs4 = SM.v(216 + par * 4, 4, p1=64)
                rsum(ss4, V(sq.t, sq.name, 0, 512, p1=64, fn=lambda a: a.rearrange("p (h e) -> p h e", h=4)))
                act(ss4, ss4, AF.Sqrt, scale=1.0 / 128, bias=CB.v(3, 1, p1=64))
                recip(rs4, ss4)
                for h in range(4):
                    stt(YHK.v(c * 512 + h * 128, 128, p1=64), psO.v(h * 128, 128, p1=64), SM.v(216 + par * 4 + h, 1, p1=64),
                        BR[l].v(512 + h * 128, 128, p1=64), ALU.mult, ALU.mult)

            def h_part2b(c):
                pT2 = psumT()
                for j in range(4):
                    tr(pT2.v(j * 64, 64), YHK.v(c * 512 + j * 128, 128, p1=64), IDB.v(0, 64, p1=64))
                j4 = lambda a, c=c: a.rearrange("p (j t) -> p j t", j=4)[:, :, c * 64:(c + 1) * 64]
                tt(YH.v(0, 2048, fn=j4), pT2.v(0, 256, fn=lambda a: a.rearrange("p (j t) -> p j t", j=4)),
                   mb(SGH_O, 2048, fn=j4), ALU.mult)
            pos_ = {}
            for c in range(8):
                pos_[c] = h_part1(c)
                if c >= 1:
                    h_part2(c - 1, pos_[c - 1])
                if c >= 2:
                    h_part2b(c - 2)
            h_part2(7, pos_[7])
            h_part2b(6)
            h_part2b(7)
            chk(3)
            wrx = wload("win", l, 4104, 512)
            wrg = wload("win", l, 4616, 512)
            for c in range(4):
                pre = tm()
                proj_fm(wrx, 512, c, 8, xbk, lambda psv, pre=pre: act(pre.v(4, T), psv, AF.Copy))
                copy(pre.v(1, 3), HALR[l].v(c * 3, 3))
                u = tm().v(0, T)
                ts(u, pre.v(4, T), cv.v(40 + c * 4 + 3, 1), cv.v(56 + c, 1), ALU.mult, ALU.add)
                for jj in (2, 1, 0):
                    stt(u, pre.v(1 + jj, T), cv.v(40 + c * 4 + jj, 1), u, ALU.mult, ALU.add)
                copy(HALR[l].v(c * 3, 3), pre.v(513, 3))
                ub = SMB.v(0, 512)
                act(ub, u, AF.Copy)
                psr = psum()
                mm(psr.v(0, T), RW[l].v(c * 128, 128), ub)
                psi = psum()
                mm(psi.v(0, T), RW[l].v(512 + c * 128, 128), ub)
                ge = tm().v(0, T)

                def ev_g(psv, ge=ge):
                    act(ge, psv, AF.Erf, scale=2.0 ** -0.5)
                    ts(ge, ge, 1.0, 0.5, ALU.add, ALU.mult)
                    tt(ge, ge, psv, ALU.mult)
                proj_fm(wrg, 512, c, 8, xbk, ev_g)
                r_ = tm().v(0, T)
                i_ = tm().v(0, T)
                act(r_, psr.v(0, T), AF.Sigmoid, bias=cv.v(60 + c, 1))
                act(i_, psi.v(0, T), AF.Sigmoid, bias=cv.v(64 + c, 1))
                a_ = tm().v(0, T)
                th = tm().v(0, T)
                act(th, r_, AF.Tanh, scale=LC[l].v(12 + c, 1))
                ts(r_, th, 1.0, None, ALU.add)
                recip(r_, r_)
                ts(a_, th, -1.0, 1.0, ALU.mult, ALU.add)
                tt(a_, a_, r_, ALU.mult)
                tt(th, th, r_, ALU.mult)
                act(a_, a_, AF.Sqrt)
                act(th, th, AF.Sqrt, scale=2.0)
                tt(i_, i_, u, ALU.mult)
                tt(i_, i_, th, ALU.mult)
                hh = tm().v(0, T)
                scan(hh, a_, i_, RST[l].v(c, 1), ALU.mult, ALU.add)
                copy(RST[l].v(c, 1), V(hh.t, hh.name, T - 1, T))
                tt(YR.v(c * 512, 512), hh, ge, ALU.mult)

            chk(4)
            for bi, (bn, Y) in enumerate((("wbm", YM), ("wbh", YH), ("wbr", YR))):
                wb_ = wload(bn, l, 0, 1024)
                for half in range(2):
                    wgt = wload("win", l, 5128 + bi * 1024 + half * 512, 512)
                    for jj in range(4):
                        j = half * 4 + jj
                        sg = tm().v(0, T)
                        proj_fm(wgt, 512, jj, 8, xbk, lambda psv, sg=sg: act(sg, psv, AF.Sigmoid))
                        psP = psum()
                        for kc in range(4):
                            mm(psP.v(0, T), wb_.v(kc * 1024 + j * 128, 128), Y.v(kc * 512, 512), start=(kc == 0), stop=(kc == 3))
                        if bi == 0:
                            tt(MIX.v(j * T, T), sg, psP.v(0, T), ALU.mult)
                        else:
                            tt(sg, sg, psP.v(0, T), ALU.mult)
                            tt(MIX.v(j * T, T), MIX.v(j * T, T), sg, ALU.add)
                        if bi == 2:
                            act(MIXB.v(j * T, T), MIX.v(j * T, T), AF.Copy)
            for half in range(2):
                wo = wload("wout", l, half * 512, 512)
                for jj in range(4):
                    j = half * 4 + jj
                    proj_fm(wo, 512, jj, 8, lambda kc: MIXB.v(kc * T, T),
                            lambda psv, j=j: stt(XT.v(j * T, T), XT.v(j * T, T), ALPHA, psv, ALU.mult, ALU.add))
            layer_norm(l, 72, 80)

            chk(5)
            for g in range(6):
                n = 512 if g < 5 else 256
                wg_ = wload("wg", l, g * 512, n)
                wu_ = wload("wu", l, g * 512, n)
                for jj in range(n // 128):
                    j = g * 4 + jj
                    sg = tm().v(0, T)
                    proj_fm(wg_, n, jj, 8, xbk, lambda psv, sg=sg: act(sg, psv, AF.Silu))
                    proj_fm(wu_, n, jj, 8, xbk, lambda psv, sg=sg, j=j: tt(mb(j * T, T), sg, psv, ALU.mult))
            for j in range(8):
                wd_ = wload("wd", l, j * 128, 128)
                proj_fm(wd_, 128, 0, 22, lambda kc: mb(kc * T, T),
                        lambda psv, j=j: stt(XT.v(j * T, T), XT.v(j * T, T), ALPHA, psv, ALU.mult, ALU.add))
            layer_norm(l, 88, 96)
          except _Stop:
            pass

        osrc = V(XT.t, XT.name, 0, 8 * T, fn=lambda a: a.rearrange("p (k n) -> p k n", k=8))
        S.dma("pool", (lambda e, osrc=osrc, t0=t0: e.dma_start(
            out=outT_d.rearrange("(k p) s -> p k s", p=128)[:, :, t0:t0 + T], in_=osrc.ap)),
            reads=[osrc], writes=[], semkey=f"o{ti % 2}")

    S.wait_all("pool", [(k, v) for k, v in S.dmacnt.items() if k.startswith("o")])

    EP = 16000
    sems = {}
    for k in sorted(S.semkeys):
        if k in Sched.ENG:
            for ep in range(S.cnt[k] // EP + 1):
                sems[(k, ep)] = es.enter_context(nc.semaphore(f"sem_{k}_{ep}"))
        else:
            sems[k] = es.enter_context(nc.semaphore(f"sem_{k}"))
    engmap = {"pe": "tensor", "act": "scalar", "dve": "vector", "pool": "gpsimd", "sp": "sync"}
    with nc.Block() as block:
        for ename, bname in engmap.items():
            def body(e, ename=ename):
                for waits, fn, inc in S.ops[ename]:
                    for k, v in waits:
                        if k in Sched.ENG:
                            e.wait_ge(sems[(k, (v - 1) // EP)], (v - 1) % EP + 1)
                        else:
                            e.wait_ge(sems[k], v)
                    if fn is not None:
                        ins = fn(e)
                        if inc is not None:
                            if inc[0] in Sched.ENG:
                                ins.then_inc(sems[(inc[0], (inc[1] - 1) // EP)], 1)
                            else:
                                ins.then_inc(sems[inc[0]], inc[1])
            getattr(block, bname)(body)
    es.close()
    stats = {k: len(v) for k, v in S.ops.items()}
    return nc, stats


def _consts():
    c = np.zeros((128, 512), np.float32)
    c[:, 0:128] = np.triu(np.ones((128, 128), np.float32))
    c[:, 128:256] = np.eye(128, dtype=np.float32)
    c[:, 256:384] = 1.0 / 1024.0
    c[:, 384:512] = 1.0
    return c


def _pack(inp, n_layers):
    f = lambda a: np.asarray(a, np.float32)
    cvs, brs, rws = [], [], []
    hlb = f(inp["h_lower_bounds"])
    for l in range(n_layers):
        cv = np.zeros((128, NCV), np.float32)
        cv[:, 0:32] = f(inp["m_conv_w"])[l].reshape(4, 8, 128).transpose(2, 1, 0).reshape(128, 32)
        cv[:, 32:40] = f(inp["m_conv_b"])[l].reshape(8, 128).T
        cv[:, 40:56] = f(inp["r_conv_w"])[l].reshape(4, 4, 128).transpose(2, 1, 0).reshape(128, 16)
        cv[:, 56:60] = f(inp["r_conv_b"])[l].reshape(4, 128).T
        cv[:, 60:64] = f(inp["r_b_rec"])[l].reshape(4, 128).T
        cv[:, 64:68] = f(inp["r_b_in"])[l].reshape(4, 128).T
        cv[:, 68:72] = f(inp["r_lambda"])[l].reshape(4, 128).T
        cv[:, 72:80] = f(inp["ln1_g"])[l].reshape(8, 128).T
        cv[:, 80:88] = f(inp["ln1_b"])[l].reshape(8, 128).T
        cv[:, 88:96] = f(inp["ln2_g"])[l].reshape(8, 128).T
        cv[:, 96:104] = f(inp["ln2_b"])[l].reshape(8, 128).T
        cv[:, 104:108] = hlb[0].reshape(4, 128).T
        cv[:, 108:112] = hlb[min(1, hlb.shape[0] - 1)].reshape(4, 128).T
        cvs.append(cv)
        br = np.zeros((128, 1056), np.float32)
        br[:, 0:512] = f(inp["m_norm_g"])[l][None, :]
        br[:, 512:1024] = f(inp["h_norm_g"])[l][None, :]
        bif = np.concatenate([f(inp["m_bias_i"])[l], f(inp["m_bias_f"])[l]])
        br[:, 1024:1056] = np.tile(bif, 4)[None, :]
        brs.append(br)
        rw = np.concatenate([f(inp["r_w_rec"])[l].transpose(1, 0, 2).reshape(128, 512),
                             f(inp["r_w_in"])[l].transpose(1, 0, 2).reshape(128, 512)], axis=1)
        rws.append(rw)
    return np.stack(cvs), np.stack(brs), np.stack(rws)


_CACHE = {}


def run(inputs, S_len, n_layers, seqs, n_cores):
    key = (S_len, n_layers)
    if key not in _CACHE:
        _CACHE[key] = build_program(S_len, n_layers)[0]
    nc = _CACHE[key]
    cv, br, rw = _pack(inputs, n_layers)
    cm = _consts()
    shared = {
        "w_in": np.ascontiguousarray(inputs["w_in"][:n_layers], np.float32),
        "w_branch_m": np.ascontiguousarray(inputs["w_branch_m"][:n_layers], np.float32),
        "w_branch_h": np.ascontiguousarray(inputs["w_branch_h"][:n_layers], np.float32),
        "w_branch_r": np.ascontiguousarray(inputs["w_branch_r"][:n_layers], np.float32),
        "w_out": np.ascontiguousarray(inputs["w_out"][:n_layers], np.float32),
        "w_ff_gate": np.ascontiguousarray(inputs["w_ff_gate"][:n_layers], np.float32),
        "w_ff_up": np.ascontiguousarray(inputs["w_ff_up"][:n_layers], np.float32),
        "w_ff_down": np.ascontiguousarray(inputs["w_ff_down"][:n_layers], np.float32),
        "cvec": cv, "brow": br, "rw": rw, "cmat": cm,
    }
    active = [0, 1, 4, 5][:len(seqs)] if n_cores == 8 and len(seqs) <= 4 else list(range(len(seqs)))
    zero = None
    in_maps = []
    for c in range(n_cores):
        if c in active or n_cores < 8:
            m = dict(shared)
            m["xT"] = np.ascontiguousarray(seqs[(active.index(c) if c in active else c) % len(seqs)].T)
        else:
            if zero is None:
                zero = {k: np.zeros_like(v) for k, v in shared.items()}
                zero["xT"] = np.zeros((D, S_len), np.float32)
            m = zero
        in_maps.append(m)
    res = run_bass_kernel_spmd(nc, in_maps, core_ids=list(range(n_cores)))
    outs = [np.ascontiguousarray(r["outT"].T) for r in res.results]
    if n_cores == 8 and len(seqs) <= 4:
        return [outs[c] for c in active]
    return outs


def kernel(**inputs):
    x = np.asarray(inputs["x"], np.float32)
    B, S_len, _ = x.shape
    outs = run(inputs, S_len, 2, [x[b] for b in range(B)], 8)
    return np.stack(outs[:B]).astype(np.float32)
```
